# Optimizing a Trainium2 kernel written in Bass

```python
import math
import jax
import jax.numpy as jnp
from jax import lax
import numpy as np

D_MODEL = 1024
BATCH = 16
SEQ = 2048
DEPTH = 4

D_FF = 2816
RMS_EPS = 1e-6

HY_WIDTH = D_MODEL // 2
HY_ORDER = 2
HY_SHORT_CONV = 3
HY_EMB = 33
HY_BANDS = (HY_EMB - 1) // 2
HY_FILTER_HIDDEN = 64
HY_DECAY_TARGET = 1e-2
HY_FAST_DECAY = 0.3
HY_SLOW_DECAY = 1.5
HY_MOD_SHIFT = 0.05
HY_NORM_EPS = 1e-6

ATTN_GROUPS = ((128, 1), (512, 4), (2048, 16))
N_GROUPS = len(ATTN_GROUPS)
HEADS_PER_GROUP = 8
HEAD_DIM = 64
N_ATTN_HEADS = N_GROUPS * HEADS_PER_GROUP
ATTN_WIDTH = HEADS_PER_GROUP * HEAD_DIM
BAND = max(w // (2 * d) for (w, d) in ATTN_GROUPS)
N_BUCKETS = 32
BUCKET_MAX_EXACT = 8
BUCKET_MAX_DIST = 1024
NEG_INF = -1e30

RG_WIDTH = D_MODEL // 2
RG_BLOCKS = 8
RG_BLOCK = RG_WIDTH // RG_BLOCKS
RG_CONV = 4
RG_CONV_LEFT = 2
RG_C = 8.0

N_BRANCH = 3
HY_COLS = (HY_ORDER + 1) * HY_WIDTH
ATTN_QKV_COLS = 3 * N_ATTN_HEADS * HEAD_DIM
RG_COLS = 2 * RG_WIDTH
IN_COLS = HY_COLS + ATTN_QKV_COLS + RG_COLS

kernel_name = "hybrid_hyena_dilated_attn_rglru_macaron"


def rms_norm(x, g):
    xf = x.astype(jnp.float32)
    y = xf * lax.rsqrt(jnp.mean(xf * xf, axis=-1, keepdims=True) + RMS_EPS)
    return (y * g.astype(jnp.float32)).astype(x.dtype)


def swiglu(x, wg, wu, wd):
    return (jax.nn.silu(x @ wg) * (x @ wu)) @ wd


def depthwise_conv(x, w, b, pad_left):
    K = w.shape[0]
    S = x.shape[1]
    xp = jnp.pad(x, ((0, 0), (pad_left, K - 1 - pad_left), (0, 0)))
    return sum(xp[:, k:k + S] * w[k] for k in range(K)) + b


def hyena_filter_spectra(L, w1, b1, w2, b2, w3, b3, freq, wout):
    f32 = jnp.float32
    t = jnp.linspace(0.0, 1.0, L, dtype=f32)[:, None]
    tr = jnp.arange(L, dtype=f32)[:, None]
    wpos = 2.0 * math.pi * tr / L
    fb = jnp.linspace(1e-4, HY_BANDS - 1, HY_BANDS, dtype=f32)[None, :]
    z = jnp.concatenate([t, jnp.cos(fb * wpos), -jnp.sin(fb * wpos)], axis=-1)
    fr = freq.astype(f32)
    hdn = jnp.sin(fr * (z @ w1.astype(f32) + b1.astype(f32)))
    hdn = jnp.sin(fr * (hdn @ w2.astype(f32) + b2.astype(f32)))
    hdn = jnp.sin(fr * (hdn @ w3.astype(f32) + b3.astype(f32)))
    h = (hdn @ wout.astype(f32)).reshape(L, 2, HY_ORDER, HY_WIDTH)
    deltas = jnp.abs(jnp.linspace(math.log(HY_DECAY_TARGET) / HY_FAST_DECAY,
                                  math.log(HY_DECAY_TARGET) / HY_SLOW_DECAY,
                                  HY_WIDTH, dtype=f32))
    decay = jnp.exp(-t * deltas[None, :]) + HY_MOD_SHIFT
    h = h * decay[:, None, None, :]
    kf = h[:, 0]
    kb = h[:, 1]
    K = jnp.concatenate([kf, jnp.zeros((1, HY_ORDER, HY_WIDTH), f32), kb[:0:-1]], axis=0)
    K = K / (jnp.sum(jnp.abs(K), axis=0, keepdims=True) + HY_NORM_EPS)
    return jnp.fft.rfft(K, axis=0)


def fft_long_conv(z, kf_spec):
    L = z.shape[1]
    Z = jnp.fft.rfft(z.astype(jnp.float32), n=2 * L, axis=1)
    y = jnp.fft.irfft(Z * kf_spec[None], n=2 * L, axis=1)[:, :L]
    return y.astype(z.dtype)


def t5_bucket(rel):
    half = N_BUCKETS // 2
    ret = jnp.where(rel > 0, half, 0)
    n = jnp.abs(rel)
    nf = jnp.maximum(n, 1).astype(jnp.float32)
    large = BUCKET_MAX_EXACT + (jnp.log(nf / BUCKET_MAX_EXACT)
                                / math.log(BUCKET_MAX_DIST / BUCKET_MAX_EXACT)
                                * (half - BUCKET_MAX_EXACT)).astype(jnp.int32)
    large = jnp.minimum(large, half - 1)
    return ret + jnp.where(n < BUCKET_MAX_EXACT, n, large)


def dilated_band_attention(q, k, v, dilation, half_span, bias_table):
    B, S, H, E = q.shape
    d = dilation
    Ls = S // d
    nb = -(-Ls // BAND)
    Lp = nb * BAND

    def to_sub(t):
        return t.reshape(B, Ls, d, H, E).transpose(0, 2, 3, 1, 4)

    pad_q = ((0, 0), (0, 0), (0, 0), (0, Lp - Ls), (0, 0))
    pad_k = ((0, 0), (0, 0), (0, 0), (BAND, Lp - Ls + BAND), (0, 0))
    qs = jnp.pad(to_sub(q), pad_q).reshape(B, d, H, nb, BAND, E)
    ks = jnp.pad(to_sub(k), pad_k).reshape(B, d, H, nb + 2, BAND, E)
    vs = jnp.pad(to_sub(v), pad_k).reshape(B, d, H, nb + 2, BAND, E)
    kb = jnp.concatenate([ks[:, :, :, 0:nb], ks[:, :, :, 1:nb + 1], ks[:, :, :, 2:nb + 2]], axis=-2)
    vb = jnp.concatenate([vs[:, :, :, 0:nb], vs[:, :, :, 1:nb + 1], vs[:, :, :, 2:nb + 2]], axis=-2)

    s = jnp.einsum('bdhnqe,bdhnke->bdhnqk', qs, kb,
                   preferred_element_type=jnp.float32) * (HEAD_DIM ** -0.5)
    qi = jnp.arange(BAND, dtype=jnp.int32)[:, None]
    kj = jnp.arange(3 * BAND, dtype=jnp.int32)[None, :]
    delta = kj - BAND - qi
    key_idx = jnp.arange(nb, dtype=jnp.int32)[:, None, None] * BAND + kj[None] - BAND
    valid = (jnp.abs(delta)[None] <= half_span) & (key_idx >= 0) & (key_idx < Ls)
    bias = bias_table.astype(jnp.float32)[t5_bucket(delta * d)]
    s = s + jnp.transpose(bias, (2, 0, 1))[None, None, :, None]
    s = jnp.where(valid[None, None, None], s, NEG_INF)
    m = jnp.max(s, axis=-1, keepdims=True)
    p = jnp.exp(s - m)
    den = jnp.sum(p, axis=-1)
    o = jnp.einsum('bdhnqk,bdhnke->bdhnqe', p, vb.astype(jnp.float32)) / den[..., None]
    lse = m[..., 0] + jnp.log(den)

    o = o.reshape(B, d, H, Lp, E)[:, :, :, :Ls].transpose(0, 3, 1, 2, 4).reshape(B, S, H, E)
    lse = lse.reshape(B, d, H, Lp)[:, :, :, :Ls].transpose(0, 3, 1, 2).reshape(B, S, H)
    return o, lse


def rg_lru_scan(xc, wa, ba, wx, bx, lam):
    B, S, _ = xc.shape
    xb = xc.reshape(B, S, RG_BLOCKS, RG_BLOCK)
    r = jax.nn.sigmoid(jnp.einsum('bshi,hij->bshj', xb, wa).reshape(B, S, RG_WIDTH) + ba)
    gi = jax.nn.sigmoid(jnp.einsum('bshi,hij->bshj', xb, wx).reshape(B, S, RG_WIDTH) + bx)
    log_a = -RG_C * r.astype(jnp.float32) * jax.nn.softplus(-lam.astype(jnp.float32))
    a = jnp.exp(log_a)
    u = jnp.sqrt(-jnp.expm1(2.0 * log_a)) * (gi * xc).astype(jnp.float32)

    def combine(e1, e2):
        a1, b1 = e1
        a2, b2 = e2
        return a1 * a2, a2 * b1 + b2

    _, h = lax.associative_scan(combine, (a, u), axis=1)
    return h


def hybrid_mixer(xn, w_in, hy_conv_w, hy_conv_b, hy_w1, hy_b1, hy_w2, hy_b2, hy_w3, hy_b3,
                 hy_freq, hy_wout, hy_skip, rel_bias, rg_conv_w, rg_conv_b, rg_wa, rg_ba,
                 rg_wx, rg_bx, rg_lambda, w_gate, b_gate, w_proj_hy, w_proj_attn, w_proj_rg, w_out):
    B, S, _ = xn.shape
    proj = xn @ w_in
    u_hy = proj[..., :HY_COLS]
    qkv = proj[..., HY_COLS:HY_COLS + ATTN_QKV_COLS]
    u_rg = proj[..., HY_COLS + ATTN_QKV_COLS:]

    uc = depthwise_conv(u_hy, hy_conv_w, hy_conv_b, (HY_SHORT_CONV - 1) // 2)
    v_hy = uc[..., :HY_WIDTH]
    gates_hy = (uc[..., HY_WIDTH:2 * HY_WIDTH], uc[..., 2 * HY_WIDTH:])
    spec = hyena_filter_spectra(S, hy_w1, hy_b1, hy_w2, hy_b2, hy_w3, hy_b3, hy_freq, hy_wout)
    z = v_hy
    for o in range(HY_ORDER):
        z = gates_hy[o] * (fft_long_conv(z, spec[:, o]) + hy_skip[o] * z)
    y_a = z

    qkv = qkv.reshape(B, S, 3, N_GROUPS, HEADS_PER_GROUP, HEAD_DIM)
    outs = []
    lses = []
    for g, (win, dil) in enumerate(ATTN_GROUPS):
        o_g, l_g = dilated_band_attention(
            qkv[:, :, 0, g], qkv[:, :, 1, g], qkv[:, :, 2, g], dil, win // (2 * dil),
            rel_bias[:, g * HEADS_PER_GROUP:(g + 1) * HEADS_PER_GROUP])
        outs.append(o_g)
        lses.append(l_g)
    wts = jax.nn.softmax(jnp.stack(lses, axis=-1), axis=-1)
    o_att = jnp.einsum('gbshe,bshg->bshe', jnp.stack(outs, axis=0), wts)
    y_b = o_att.reshape(B, S, ATTN_WIDTH).astype(xn.dtype)

    x_rg = u_rg[..., :RG_WIDTH]
    gate_rg = u_rg[..., RG_WIDTH:]
    xc = depthwise_conv(x_rg, rg_conv_w, rg_conv_b, RG_CONV_LEFT)
    h_f = rg_lru_scan(xc, rg_wa[0], rg_ba[0], rg_wx[0], rg_bx[0], rg_lambda[0])
    h_b = jnp.flip(rg_lru_scan(jnp.flip(xc, axis=1), rg_wa[1], rg_ba[1], rg_wx[1],
                               rg_bx[1], rg_lambda[1]), axis=1)
    y_c = (h_f + h_b).astype(xn.dtype) * jax.nn.gelu(gate_rg)

    gates = jax.nn.sigmoid(xn @ w_gate + b_gate).reshape(B, S, N_BRANCH, D_MODEL)
    merged = (gates[:, :, 0] * (y_a @ w_proj_hy)
              + gates[:, :, 1] * (y_b @ w_proj_attn)
              + gates[:, :, 2] * (y_c @ w_proj_rg))
    return merged @ w_out


def setup_inputs(seed: int = 0) -> dict:
    key = jax.random.key(seed)
    ks = iter(jax.random.split(key, 64))
    f32 = jnp.float32
    L = DEPTH

    def nrm(shape, scale):
        return jax.random.normal(next(ks), shape, f32) * scale

    def gain(shape):
        return 1.0 + nrm(shape, 0.05)

    lam_u = jax.random.uniform(next(ks), (L, 2, RG_WIDTH), f32, 0.9, 0.999) ** (1.0 / RG_C)
    inputs = {
        "x": nrm((BATCH, SEQ, D_MODEL), 1.0),
        "ffn1_norm": gain((L, D_MODEL)),
        "ffn1_wg": nrm((L, D_MODEL, D_FF), D_MODEL ** -0.5),
        "ffn1_wu": nrm((L, D_MODEL, D_FF), D_MODEL ** -0.5),
        "ffn1_wd": nrm((L, D_FF, D_MODEL), D_FF ** -0.5),
        "mix_norm": gain((L, D_MODEL)),
        "w_in": nrm((L, D_MODEL, IN_COLS), D_MODEL ** -0.5),
        "hy_conv_w": nrm((L, HY_SHORT_CONV, HY_COLS), HY_SHORT_CONV ** -0.5),
        "hy_conv_b": nrm((L, HY_COLS), 0.02),
        "hy_w1": nrm((L, HY_EMB, HY_FILTER_HIDDEN), HY_EMB ** -0.5),
        "hy_b1": nrm((L, HY_FILTER_HIDDEN), 0.1),
        "hy_w2": nrm((L, HY_FILTER_HIDDEN, HY_FILTER_HIDDEN), HY_FILTER_HIDDEN ** -0.5),
        "hy_b2": nrm((L, HY_FILTER_HIDDEN), 0.1),
        "hy_w3": nrm((L, HY_FILTER_HIDDEN, HY_FILTER_HIDDEN), HY_FILTER_HIDDEN ** -0.5),
        "hy_b3": nrm((L, HY_FILTER_HIDDEN), 0.1),
        "hy_freq": gain((L, HY_FILTER_HIDDEN)),
        "hy_wout": nrm((L, HY_FILTER_HIDDEN, 2 * HY_ORDER * HY_WIDTH), HY_FILTER_HIDDEN ** -0.5),
        "hy_skip": nrm((L, HY_ORDER, HY_WIDTH), 0.5),
        "rel_bias": nrm((N_BUCKETS, N_ATTN_HEADS), 0.2),
        "rg_conv_w": nrm((L, RG_CONV, RG_WIDTH), RG_CONV ** -0.5),
        "rg_conv_b": nrm((L, RG_WIDTH), 0.02),
        "rg_wa": nrm((L, 2, RG_BLOCKS, RG_BLOCK, RG_BLOCK), RG_BLOCK ** -0.5),
        "rg_ba": nrm((L, 2, RG_WIDTH), 0.02),
        "rg_wx": nrm((L, 2, RG_BLOCKS, RG_BLOCK, RG_BLOCK), RG_BLOCK ** -0.5),
        "rg_bx": nrm((L, 2, RG_WIDTH), 0.02),
        "rg_lambda": jnp.log(lam_u / (1.0 - lam_u)),
        "w_gate": nrm((L, D_MODEL, N_BRANCH * D_MODEL), D_MODEL ** -0.5),
        "b_gate": nrm((L, N_BRANCH * D_MODEL), 0.02),
        "w_proj_hy": nrm((L, HY_WIDTH, D_MODEL), HY_WIDTH ** -0.5),
        "w_proj_attn": nrm((L, ATTN_WIDTH, D_MODEL), ATTN_WIDTH ** -0.5),
        "w_proj_rg": nrm((L, RG_WIDTH, D_MODEL), RG_WIDTH ** -0.5),
        "w_out": nrm((L, D_MODEL, D_MODEL), D_MODEL ** -0.5),
        "ffn2_norm": gain((L, D_MODEL)),
        "ffn2_wg": nrm((L, D_MODEL, D_FF), D_MODEL ** -0.5),
        "ffn2_wu": nrm((L, D_MODEL, D_FF), D_MODEL ** -0.5),
        "ffn2_wd": nrm((L, D_FF, D_MODEL), D_FF ** -0.5),
        "final_norm": gain((D_MODEL,)),
    }
    return inputs


def reference(x, ffn1_norm, ffn1_wg, ffn1_wu, ffn1_wd, mix_norm, w_in, hy_conv_w, hy_conv_b,
              hy_w1, hy_b1, hy_w2, hy_b2, hy_w3, hy_b3, hy_freq, hy_wout, hy_skip, rel_bias,
              rg_conv_w, rg_conv_b, rg_wa, rg_ba, rg_wx, rg_bx, rg_lambda, w_gate, b_gate,
              w_proj_hy, w_proj_attn, w_proj_rg, w_out, ffn2_norm, ffn2_wg, ffn2_wu, ffn2_wd,
              final_norm):
    h = x
    for l in range(DEPTH):
        h = h + 0.5 * swiglu(rms_norm(h, ffn1_norm[l]), ffn1_wg[l], ffn1_wu[l], ffn1_wd[l])
        h = h + hybrid_mixer(
            rms_norm(h, mix_norm[l]), w_in[l], hy_conv_w[l], hy_conv_b[l],
            hy_w1[l], hy_b1[l], hy_w2[l], hy_b2[l], hy_w3[l], hy_b3[l], hy_freq[l], hy_wout[l],
            hy_skip[l], rel_bias, rg_conv_w[l], rg_conv_b[l], rg_wa[l], rg_ba[l], rg_wx[l],
            rg_bx[l], rg_lambda[l], w_gate[l], b_gate[l], w_proj_hy[l], w_proj_attn[l],
            w_proj_rg[l], w_out[l])
        h = h + 0.5 * swiglu(rms_norm(h, ffn2_norm[l]), ffn2_wg[l], ffn2_wu[l], ffn2_wd[l])
    return rms_norm(h, final_norm)
```

```python
import math
from contextlib import ExitStack

import numpy as np
import ml_dtypes
import concourse.bass as bass
import concourse.mybir as mybir
from concourse.bass_utils import run_bass_kernel_spmd

F32 = mybir.dt.float32
BF16 = mybir.dt.bfloat16
AF = mybir.ActivationFunctionType
ALU = mybir.AluOpType

D = 1024
T = 2048
DFF = 2816
NF = 17
GROUPS = ((128, 1), (512, 4), (2048, 16))
PC = dict(n1=0, nm=8, n2=16, bg=24, hcw=48, hcb=84, rcw=96, rcb=112, rba=116, rbx=124, rlam=132,
          hb1=140, hb2=141, hb3=142, hfr=143)
NPC = 144


class Sem:
    def __init__(self, h):
        self.h = h
        self.total = 0


class Eng:
    def __init__(self, name, eng, sem, is_pe=False):
        self.name = name
        self.eng = eng
        self.sem = sem
        self.is_pe = is_pe
        self.seen = {}
        self.dirty = False


class Buf:
    __slots__ = ("w", "r")

    def __init__(self):
        self.w = None
        self.r = {}


class Tile:
    def __init__(self, t):
        self.t = t
        self.b = Buf()


class Builder:
    def __init__(self, n_seq, layers, flags=None, nl=4):
        self.nl = nl
        self.n_seq = n_seq
        self.layers = layers
        self.flags = flags or {}
        self.nc = bass.Bass("TRN2", target_bir_lowering=False)
        self.es = ExitStack()
        self.nsem = 0
        nc = self.nc
        self.pe = Eng("pe", nc.tensor, self.mksem(), True)
        self.act = Eng("act", nc.scalar, self.mksem())
        self.dve = Eng("dve", nc.vector, self.mksem())
        self.pool = Eng("pool", nc.gpsimd, self.mksem())
        self.sp = Eng("sp", nc.sync, self.mksem())
        self.dsems = [self.mksem() for _ in range(24)]
        self.dnext = 0

    def mksem(self):
        self.nsem += 1
        return Sem(self.es.enter_context(self.nc.semaphore(f"s{self.nsem}")))

    def _wait(self, E, deps):
        for sem, val in deps:
            if val <= 0 or E.seen.get(sem, 0) >= val:
                continue
            assert val <= sem.total, f"wait on un-issued milestone ({E.name})"
            E.eng.wait_ge(sem.h, val)
            E.seen[sem] = val

    def op(self, E, fn, reads=(), writes=(), inc=True):
        deps = []
        for b in reads:
            if b.w is not None and not (E.is_pe and b.w[0] is E.sem):
                deps.append(b.w)
        for b in writes:
            if b.w is not None and not (E.is_pe and b.w[0] is E.sem):
                deps.append(b.w)
            for sem, val in b.r.items():
                if E.is_pe and sem is E.sem:
                    continue
                deps.append((sem, val))
        self._wait(E, deps)
        inst = fn()
        if inc:
            E.sem.total += 1
            inst.then_inc(E.sem.h, 1)
            st = (E.sem, E.sem.total)
            E.dirty = False
        else:
            st = (E.sem, E.sem.total + 1)
            E.dirty = True
        for b in reads:
            b.r[st[0]] = max(b.r.get(st[0], 0), st[1])
        for b in writes:
            b.w = st
            b.r = {}
        return inst

    def dma(self, Q, out, in_, reads=(), writes=()):
        sem = self.dsems[self.dnext]
        self.dnext = (self.dnext + 1) % len(self.dsems)
        deps = [(sem, sem.total)]
        for b in reads:
            if b.w is not None:
                deps.append(b.w)
        for b in writes:
            if b.w is not None:
                deps.append(b.w)
            deps.extend(b.r.items())
        self._wait(Q, deps)
        Q.eng.dma_start(out=out, in_=in_).then_inc(sem.h, 16)
        sem.total += 16
        st = (sem, sem.total)
        for b in reads:
            b.r[sem] = st[1]
        for b in writes:
            b.w = st
            b.r = {}

    def barrier(self):
        comp = [self.pe, self.act, self.dve, self.pool]
        assert not self.pe.dirty
        for E in [self.pe, self.act, self.dve, self.pool, self.sp]:
            self._wait(E, [(X.sem, X.sem.total) for X in comp if not (X is E and E.is_pe)])
        for E in comp:
            if E.sem.total > 24000:
                E.sem = self.mksem()

    def sb(self, st, shape, dt):
        self.ntile = getattr(self, "ntile", 0) + 1
        return Tile(st.enter_context(self.nc.sbuf_tensor(f"t{self.ntile}", list(shape), dt)))

    def build(self):
        nc = self.nc
        g = self.es
        dr = lambda n, s, dt=F32, kind="ExternalInput": nc.dram_tensor(n, list(s), dt, kind=kind).ap()
        NS = self.n_seq
        NL = self.nl
        self.xT = dr("xT", [NS, D, T])
        self.yT = dr("yT", [NS, D, T], kind="ExternalOutput")
        self.w = {}
        for f in ("ffn1", "ffn2"):
            self.w[f + "_wg"] = dr(f + "_wg", [NL, D, DFF])
            self.w[f + "_wu"] = dr(f + "_wu", [NL, D, DFF])
            self.w[f + "_wd"] = dr(f + "_wd", [NL, DFF, D])
        self.w["w_in"] = dr("w_in", [NL, D, 7168])
        self.w["w_gate"] = dr("w_gate", [NL, D, 3072])
        for n in ("w_proj_hy", "w_proj_attn", "w_proj_rg"):
            self.w[n] = dr(n, [NL, 512, D])
        self.w["w_out"] = dr("w_out", [NL, D, D])
        self.d_pcol = dr("pcol", [NL, 128, NPC])
        self.d_fn = dr("fncol", [128, 8])
        self.d_w1 = dr("hy_w1", [NL, 33, 64])
        self.d_w2 = dr("hy_w2", [NL, 64, 64])
        self.d_w3 = dr("hy_w3", [NL, 64, 64])
        self.d_wout = dr("hy_wout", [NL, 64, 2048])
        self.d_zT = dr("zT", [33, T])
        self.d_delta = dr("delta_bc", [128, 512])
        self.d_tcol = dr("tcol", [128, 16])
        self.d_skip = dr("skip_bc", [NL, 2, 128, 512])
        self.d_bd = dr("rg_bd", [NL, 2, 2, 4, 128, 128])
        self.d_ab = dr("attn_bias", [3, 128, 8, 256])
        self.d_csf = dr("csf", [NF, 128, 2, 16, 128], BF16)
        self.d_csi = dr("csi", [NF, 2, 128, 2, 1024], BF16)
        self.d_ident = dr("ident", [128, 128], BF16)

        self.h = self.sb(g, [128, 8, T], F32)
        self.hB = [[Buf() for _ in range(4)] for _ in range(8)]
        self.xn = self.sb(g, [128, 8, T], BF16)
        self.xnB = [[Buf() for _ in range(4)] for _ in range(8)]
        self.ones32 = self.sb(g, [128, 128], F32)
        self.onesb = self.sb(g, [128, 3, 64], BF16)
        self.ident = self.sb(g, [128, 128], BF16)
        self.pcol = self.sb(g, [128, NPC], F32)
        self.fncol = self.sb(g, [128, 8], F32)
        self.sq = [self.sb(g, [128, 512], F32) for _ in range(2)]
        self.rstd2 = [self.sb(g, [128, 512], F32) for _ in range(2)]
        self.acc = Tile(g.enter_context(nc.psum_tensor("acc", [128, 2048], F32)))
        self.ps = []
        for i in range(4):
            tl = Tile(None)
            tl.ap = self.acc.t[:, i * 512:(i + 1) * 512]
            self.ps.append(tl)
        for i in range(4, 8):
            tl = Tile(g.enter_context(nc.psum_tensor(f"ps{i}", [128, 512], F32)))
            tl.ap = tl.t[:]
            self.ps.append(tl)

        self.op(self.dve, lambda: nc.vector.memset(self.ones32.t[:], 1.0), writes=[self.ones32.b])
        self.op(self.dve, lambda: nc.vector.memset(self.onesb.t[:], 1.0), writes=[self.onesb.b])
        self.op(self.dve, lambda: nc.vector.memset(self.onesb.t[0:64, 0, :], 0.0), writes=[self.onesb.b])
        self.op(self.dve, lambda: nc.vector.memset(self.onesb.t[64:128, 2, :], 0.0), writes=[self.onesb.b])
        self.dma(self.sp, self.ident.t[:], self.d_ident[:, :], writes=[self.ident.b])
        self.dma(self.sp, self.fncol.t[:], self.d_fn[:, :], writes=[self.fncol.b])

        for s in range(NS):
            for kc in range(8):
                self.dma(self.sp, self.h.t[:, kc, :], self.xT[s, kc * 128:(kc + 1) * 128, :], writes=self.hB[kc])
            for l in self.layers:
                self.dma(self.sp, self.pcol.t[:], self.d_pcol[l], writes=[self.pcol.b])
                if self.flags.get("ffn1", True):
                    self.ffn(l, "ffn1", PC["n1"])
                if self.flags.get("mix", True):
                    self.mixer(l)
                if self.flags.get("ffn2", True):
                    self.ffn(l, "ffn2", PC["n2"])
                self.barrier()
            self.final(s)
            self.barrier()
        self._wait(self.sp, [(sm, sm.total) for sm in self.dsems])
        return nc

    def rmsnorm(self, gcol, out_fn=None, save=None, load=None):
        nc = self.nc
        if load is not None:
            for tt in range(4):
                sl = slice(tt * 512, (tt + 1) * 512)
                for kc in range(8):
                    self.op(self.dve, lambda: nc.vector.scalar_tensor_tensor(
                        out=self.xn.t[:, kc, sl], in0=self.h.t[:, kc, sl], scalar=gcol[:, kc:kc + 1],
                        in1=load.t[:, sl], op0=ALU.mult, op1=ALU.mult),
                        reads=[self.hB[kc][tt], load.b, self.pcol.b], writes=[self.xnB[kc][tt]])
            return
        for tt in range(4):
            sl = slice(tt * 512, (tt + 1) * 512)
            pst = self.ps[6 + tt % 2]
            rs = self.rstd2[tt % 2]
            for kc in range(8):
                sq = self.sq[kc % 2]
                self.op(self.act, lambda: nc.scalar.activation(out=sq.t[:], in_=self.h.t[:, kc, sl], func=AF.Square),
                        reads=[self.hB[kc][tt]], writes=[sq.b])
                self.op(self.pe, lambda: nc.tensor.matmul(pst.ap, lhsT=self.ones32.t[:], rhs=sq.t[:],
                                                          start=(kc == 0), stop=(kc == 7)),
                        reads=[sq.b, self.ones32.b], writes=[pst.b])
            self.op(self.act, lambda: nc.scalar.activation(out=rs.t[:], in_=pst.ap, func=AF.Ln,
                                                           bias=self.eps.t[:, 0:1], scale=1.0 / D),
                    reads=[pst.b, self.eps.b], writes=[rs.b])
            self.op(self.act, lambda: nc.scalar.activation(out=rs.t[:], in_=rs.t[:], func=AF.Exp, scale=-0.5),
                    reads=[rs.b], writes=[rs.b])
            if save is not None:
                self.op(self.act, lambda: nc.scalar.activation(out=save.t[:, sl], in_=rs.t[:], func=AF.Copy),
                        reads=[rs.b], writes=[save.b])
            for kc in range(8):
                if out_fn is None:
                    self.op(self.dve, lambda: nc.vector.scalar_tensor_tensor(
                        out=self.xn.t[:, kc, sl], in0=self.h.t[:, kc, sl], scalar=gcol[:, kc:kc + 1],
                        in1=rs.t[:], op0=ALU.mult, op1=ALU.mult),
                        reads=[self.hB[kc][tt], rs.b, self.pcol.b], writes=[self.xnB[kc][tt]])
                else:
                    out_fn(kc, tt, sl, rs)

    def final(self, s):
        nc = self.nc
        if not self.flags.get("final", True):
            for kc in range(8):
                self.dma(self.sp, self.yT[s, kc * 128:(kc + 1) * 128, :], self.h.t[:, kc, :], reads=self.hB[kc])
            return
        with ExitStack() as ph:
            st = [self.sb(ph, [128, 512], F32) for _ in range(3)]
            cnt = [0]

            def out_fn(kc, tt, sl, rs):
                o = st[cnt[0] % 3]
                cnt[0] += 1
                self.op(self.dve, lambda: nc.vector.scalar_tensor_tensor(
                    out=o.t[:], in0=self.h.t[:, kc, sl], scalar=self.fncol.t[:, kc:kc + 1],
                    in1=rs.t[:], op0=ALU.mult, op1=ALU.mult),
                    reads=[self.hB[kc][tt], rs.b, self.fncol.b], writes=[o.b])
                self.dma(self.sp, self.yT[s, kc * 128:(kc + 1) * 128, sl], o.t[:], reads=[o.b])

            self.rmsnorm(None, out_fn)
            self.barrier()
            self._wait(self.sp, [(sm, sm.total) for sm in self.dsems])
            for E in (self.pe, self.act, self.dve, self.pool):
                pass

    def ffn(self, l, name, ncol):
        nc = self.nc
        wg_d, wu_d, wd_d = self.w[name + "_wg"], self.w[name + "_wu"], self.w[name + "_wd"]
        self.rmsnorm(self.pcol.t[:, ncol:ncol + 8])
        pieces = [(0, 4), (4, 4), (8, 4), (12, 4), (16, 4), (20, 2)]
        with ExitStack() as ph:
            slots = []
            for _ in range(2):
                slots.append(dict(wg=self.sb(ph, [128, 8, 512], BF16), wu=self.sb(ph, [128, 8, 512], BF16),
                                  wd=self.sb(ph, [128, 4, D], BF16)))
            hT = [self.sb(ph, [128, 4, 512], BF16) for _ in range(2)]
            hTB = [[Buf() for _ in range(4)] for _ in range(2)]
            sg = [self.sb(ph, [128, 512], F32) for _ in range(2)]

            def load(pi):
                j0, nj = pieces[pi]
                sl = slots[pi % 2]
                ncl = nj * 128
                self.dma(self.pool, sl["wg"].t[:, :, 0:ncl],
                         wg_d[l].rearrange("(kc p) c -> p kc c", p=128)[:, :, j0 * 128:j0 * 128 + ncl],
                         writes=[sl["wg"].b])
                self.dma(self.pool, sl["wu"].t[:, :, 0:ncl],
                         wu_d[l].rearrange("(kc p) c -> p kc c", p=128)[:, :, j0 * 128:j0 * 128 + ncl],
                         writes=[sl["wu"].b])
                self.dma(self.pool, sl["wd"].t[:, 0:nj, :],
                         wd_d[l, j0 * 128:(j0 + nj) * 128, :].rearrange("(j p) m -> p j m", p=128),
                         writes=[sl["wd"].b])

            load(0)
            load(1)
            cg = [0]
            units = [(pi, tt) for pi in range(len(pieces)) for tt in range(4)]

            def gu(k):
                pi, tt = units[k]
                j0, nj = pieces[pi]
                sl = slots[pi % 2]
                tsl = slice(tt * 512, (tt + 1) * 512)
                ht = hT[k % 2]
                hb = hTB[k % 2]
                for j in range(nj):
                    pg = self.ps[(2 * cg[0]) % 4]
                    pu = self.ps[(2 * cg[0] + 1) % 4]
                    sgt = sg[cg[0] % 2]
                    cg[0] += 1
                    for kc in range(8):
                        self.op(self.pe, lambda: nc.tensor.matmul(
                            pg.ap, lhsT=sl["wg"].t[:, kc, j * 128:(j + 1) * 128], rhs=self.xn.t[:, kc, tsl],
                            start=(kc == 0), stop=(kc == 7)),
                            reads=[sl["wg"].b, self.xnB[kc][tt]], writes=[pg.b], inc=(kc == 7))
                    for kc in range(8):
                        self.op(self.pe, lambda: nc.tensor.matmul(
                            pu.ap, lhsT=sl["wu"].t[:, kc, j * 128:(j + 1) * 128], rhs=self.xn.t[:, kc, tsl],
                            start=(kc == 0), stop=(kc == 7)),
                            reads=[sl["wu"].b, self.xnB[kc][tt]], writes=[pu.b], inc=(kc == 7))
                    self.op(self.act, lambda: nc.scalar.activation(out=sgt.t[:], in_=pg.ap, func=AF.Silu),
                            reads=[pg.b], writes=[sgt.b])
                    self.op(self.dve, lambda: nc.vector.tensor_tensor(out=ht.t[:, j, :], in0=pu.ap, in1=sgt.t[:],
                                                                      op=ALU.mult),
                            reads=[pu.b, sgt.b], writes=[hb[j]])

            def down(k):
                pi, tt = units[k]
                j0, nj = pieces[pi]
                sl = slots[pi % 2]
                tsl = slice(tt * 512, (tt + 1) * 512)
                ht = hT[k % 2]
                hb = hTB[k % 2]
                for m in range(8):
                    po = self.ps[4 + m % 2]
                    for j in range(nj):
                        self.op(self.pe, lambda: nc.tensor.matmul(
                            po.ap, lhsT=sl["wd"].t[:, j, m * 128:(m + 1) * 128], rhs=ht.t[:, j, :],
                            start=(j == 0), stop=(j == nj - 1)),
                            reads=[sl["wd"].b, hb[j]], writes=[po.b], inc=(j == nj - 1))
                    self.op(self.dve, lambda: nc.vector.scalar_tensor_tensor(
                        out=self.h.t[:, m, tsl], in0=po.ap, scalar=0.5, in1=self.h.t[:, m, tsl],
                        op0=ALU.mult, op1=ALU.add),
                        reads=[po.b, self.hB[m][tt]], writes=[self.hB[m][tt]])
                if tt == 3 and pi + 2 < len(pieces):
                    load(pi + 2)

            for k in range(len(units) + 1):
                if k < len(units):
                    gu(k)
                if k >= 1:
                    down(k - 1)
            self.barrier()

    def mixer(self, l):
        fl = self.flags
        if fl.get("hy", True):
            self.branch_hy(l)
        else:
            self.rmsnorm(self.pcol.t[:, PC["nm"]:PC["nm"] + 8])
        if fl.get("rg", True):
            self.branch_rg(l)
        if fl.get("attn", True):
            self.branch_attn(l)

    def epilogue(self, l, b, ph, y, yB, wproj_name):
        nc = self.nc
        with ExitStack() as ep:
            wp = self.sb(ep, [128, 4, D], BF16)
            wgt = self.sb(ep, [128, 8, D], BF16)
            wo = self.sb(ep, [128, 8, D], BF16)
            tmp = [self.sb(ep, [128, 8, 512], BF16) for _ in range(2)]
            tmpB = [[Buf() for _ in range(8)] for _ in range(2)]
            gt = [self.sb(ep, [128, 512], F32) for _ in range(2)]
            wpB = [Buf() for _ in range(8)]
            wgB = [Buf() for _ in range(8)]
            woB = [Buf() for _ in range(8)]
            wpd = self.w[wproj_name][l].rearrange("(kc p) m -> p kc m", p=128)
            wgd = self.w["w_gate"][l].rearrange("(kc p) m -> p kc m", p=128)
            wod = self.w["w_out"][l].rearrange("(kc p) m -> p kc m", p=128)
            for m in range(8):
                msl = slice(m * 128, (m + 1) * 128)
                self.dma(self.pool, wp.t[:, :, msl], wpd[:, :, msl], writes=[wpB[m]])
                self.dma(self.pool, wgt.t[:, :, msl], wgd[:, :, b * D + m * 128: b * D + (m + 1) * 128], writes=[wgB[m]])
            for m in range(8):
                msl = slice(m * 128, (m + 1) * 128)
                self.dma(self.pool, wo.t[:, :, msl], wod[:, :, msl], writes=[woB[m]])
            for tt in range(4):
                tsl = slice(tt * 512, (tt + 1) * 512)
                tm = tmp[tt % 2]
                tb = tmpB[tt % 2]
                for m in range(8):
                    pP = self.ps[m % 2]
                    pG = self.ps[2 + m % 2]
                    g_ = gt[m % 2]
                    for kc in range(4):
                        self.op(self.pe, lambda: nc.tensor.matmul(
                            pP.ap, lhsT=wp.t[:, kc, m * 128:(m + 1) * 128], rhs=y.t[:, kc, tsl],
                            start=(kc == 0), stop=(kc == 3)), reads=[wpB[m], yB[kc][tt]], writes=[pP.b], inc=(kc == 3))
                    for kc in range(8):
                        self.op(self.pe, lambda: nc.tensor.matmul(
                            pG.ap, lhsT=wgt.t[:, kc, m * 128:(m + 1) * 128], rhs=self.xn.t[:, kc, tsl],
                            start=(kc == 0), stop=(kc == 7)), reads=[wgB[m], self.xnB[kc][tt]], writes=[pG.b],
                            inc=(kc == 7))
                    bcol = PC["bg"] + b * 8 + m
                    self.op(self.act, lambda: nc.scalar.activation(out=g_.t[:], in_=pG.ap, func=AF.Sigmoid,
                                                                   bias=self.pcol.t[:, bcol:bcol + 1], scale=1.0),
                            reads=[pG.b, self.pcol.b], writes=[g_.b])
                    self.op(self.dve, lambda: nc.vector.tensor_tensor(out=tm.t[:, m, :], in0=pP.ap, in1=g_.t[:],
                                                                      op=ALU.mult),
                            reads=[pP.b, g_.b], writes=[tb[m]])
                for mo in range(8):
                    pO = self.ps[4 + mo % 2]
                    for m in range(8):
                        self.op(self.pe, lambda: nc.tensor.matmul(
                            pO.ap, lhsT=wo.t[:, m, mo * 128:(mo + 1) * 128], rhs=tm.t[:, m, :],
                            start=(m == 0), stop=(m == 7)), reads=[woB[mo], tb[m]], writes=[pO.b], inc=(m == 7))
                    self.op(self.dve, lambda: nc.vector.tensor_tensor(out=self.h.t[:, mo, tsl], in0=pO.ap,
                                                                      in1=self.h.t[:, mo, tsl], op=ALU.add),
                            reads=[pO.b, self.hB[mo][tt]], writes=[self.hB[mo][tt]])
            self.barrier()

    def branch_rg(self, l):
        nc = self.nc
        with ExitStack() as ph:
            yc = self.sb(ph, [128, 4, T], BF16)
            ycB = [[Buf() for _ in range(4)] for _ in range(4)]
            with ExitStack() as p1:
                wx = [self.sb(p1, [128, 8, 128], BF16) for _ in range(2)]
                wgt = [self.sb(p1, [128, 8, 128], BF16) for _ in range(2)]
                bd = [self.sb(p1, [128, 4, 128], BF16) for _ in range(2)]
                bdB = [[Buf() for _ in range(4)] for _ in range(2)]
                xpad = self.sb(p1, [128, T + 4], F32)
                xc = self.sb(p1, [128, T], F32)
                xcb = self.sb(p1, [128, T], BF16)
                ra = self.sb(p1, [128, T], F32)
                gu = self.sb(p1, [128, T], F32)
                t1 = self.sb(p1, [128, T], F32)
                hs = [self.sb(p1, [128, T], F32) for _ in range(2)]
                gg = self.sb(p1, [128, T], BF16)
                gx = [self.sb(p1, [128, 512], F32) for _ in range(2)]
                gi_ = [self.sb(p1, [128, 512], F32) for _ in range(2)]
                ccol = self.sb(p1, [128, 8], F32)
                lam = self.pcol.t[:, PC["rlam"]:PC["rlam"] + 8]
                self.op(self.act, lambda: nc.scalar.activation(out=ccol.t[:], in_=lam, func=AF.Exp, scale=-1.0),
                        reads=[self.pcol.b], writes=[ccol.b])
                self.op(self.act, lambda: nc.scalar.activation(out=ccol.t[:], in_=ccol.t[:], func=AF.Ln,
                                                               bias=self.one.t[:, 0:1], scale=1.0),
                        reads=[ccol.b, self.one.b], writes=[ccol.b])
                self.op(self.dve, lambda: nc.vector.tensor_scalar(out=ccol.t[:], in0=ccol.t[:], scalar1=-8.0,
                                                                  scalar2=None, op0=ALU.mult),
                        reads=[ccol.b], writes=[ccol.b])
                self.op(self.dve, lambda: nc.vector.memset(xpad.t[:, 0:2], 0.0), writes=[xpad.b])
                self.op(self.dve, lambda: nc.vector.memset(xpad.t[:, T + 2:T + 4], 0.0), writes=[xpad.b])
                base = 1536 + 4608
                win = self.w["w_in"][l].rearrange("(kc p) c -> p kc c", p=128)

                def load(cc):
                    self.dma(self.pool, wx[cc % 2].t[:], win[:, :, base + cc * 128: base + (cc + 1) * 128],
                             writes=[wx[cc % 2].b])
                    self.dma(self.pool, wgt[cc % 2].t[:], win[:, :, base + 512 + cc * 128: base + 512 + (cc + 1) * 128],
                             writes=[wgt[cc % 2].b])
                    for i, (w_, d_) in enumerate(((0, 0), (0, 1), (1, 0), (1, 1))):
                        self.dma(self.pool, bd[cc % 2].t[:, i, :], self.d_bd[l, w_, d_, cc], writes=[bdB[cc % 2][i]])

                load(0)
                for cc in range(4):
                    if cc + 1 < 4:
                        load(cc + 1)
                    wx_, wg_, bd_ = wx[cc % 2], wgt[cc % 2], bd[cc % 2]
                    for tt in range(4):
                        tsl = slice(tt * 512, (tt + 1) * 512)
                        pX = self.ps[tt % 2]
                        pG = self.ps[2 + tt % 2]
                        for kc in range(8):
                            self.op(self.pe, lambda: nc.tensor.matmul(pX.ap, lhsT=wx_.t[:, kc, :], rhs=self.xn.t[:, kc, tsl],
                                                                      start=(kc == 0), stop=(kc == 7)),
                                    reads=[wx_.b, self.xnB[kc][tt]], writes=[pX.b], inc=(kc == 7))
                        for kc in range(8):
                            self.op(self.pe, lambda: nc.tensor.matmul(pG.ap, lhsT=wg_.t[:, kc, :], rhs=self.xn.t[:, kc, tsl],
                                                                      start=(kc == 0), stop=(kc == 7)),
                                    reads=[wg_.b, self.xnB[kc][tt]], writes=[pG.b], inc=(kc == 7))
                        self.op(self.act, lambda: nc.scalar.activation(out=xpad.t[:, 2 + tt * 512: 2 + (tt + 1) * 512],
                                                                       in_=pX.ap, func=AF.Copy),
                                reads=[pX.b], writes=[xpad.b])
                        gx_ = gx[tt % 2]
                        gq = gi_[tt % 2]
                        self.op(self.act, lambda: nc.scalar.activation(out=gx_.t[:], in_=pG.ap, func=AF.Copy),
                                reads=[pG.b], writes=[gx_.b])
                        self.op(self.pool, lambda: nc.gpsimd.tensor_tensor(out=gq.t[:], in0=gx_.t[:], in1=gx_.t[:], op=ALU.mult),
                                reads=[gx_.b], writes=[gq.b])
                        self.op(self.pool, lambda: nc.gpsimd.tensor_scalar(out=gq.t[:], in0=gq.t[:], scalar1=0.044715,
                                                                           scalar2=1.0, op0=ALU.mult, op1=ALU.add),
                                reads=[gq.b], writes=[gq.b])
                        self.op(self.pool, lambda: nc.gpsimd.tensor_tensor(out=gq.t[:], in0=gq.t[:], in1=gx_.t[:], op=ALU.mult),
                                reads=[gq.b, gx_.b], writes=[gq.b])
                        self.op(self.act, lambda: nc.scalar.activation(out=gq.t[:], in_=gq.t[:], func=AF.Sigmoid,
                                                                       scale=1.5957691216057308),
                                reads=[gq.b], writes=[gq.b])
                        self.op(self.pool, lambda: nc.gpsimd.tensor_tensor(out=gg.t[:, tsl], in0=gq.t[:], in1=gx_.t[:], op=ALU.mult),
                                reads=[gq.b, gx_.b], writes=[gg.b])
                    cw = lambda k: self.pcol.t[:, PC["rcw"] + k * 4 + cc: PC["rcw"] + k * 4 + cc + 1]
                    cb = self.pcol.t[:, PC["rcb"] + cc: PC["rcb"] + cc + 1]
                    self.op(self.dve, lambda: nc.vector.tensor_scalar(out=xc.t[:], in0=xpad.t[:, 0:T], scalar1=cw(0),
                                                                      scalar2=cb, op0=ALU.mult, op1=ALU.add),
                            reads=[xpad.b, self.pcol.b], writes=[xc.b])
                    for k in range(1, 4):
                        self.op(self.dve, lambda: nc.vector.scalar_tensor_tensor(
                            out=xc.t[:], in0=xpad.t[:, k:k + T], scalar=cw(k), in1=xc.t[:], op0=ALU.mult, op1=ALU.add),
                            reads=[xpad.b, xc.b, self.pcol.b], writes=[xc.b])
                    self.op(self.act, lambda: nc.scalar.activation(out=xcb.t[:], in_=xc.t[:], func=AF.Copy),
                            reads=[xc.b], writes=[xcb.b])
                    for dr_ in range(2):
                        order = [0, 1, 2, 3] if dr_ == 0 else [3, 2, 1, 0]
                        raB = [Buf() for _ in range(4)]
                        guB = [Buf() for _ in range(4)]
                        t1B = [Buf() for _ in range(4)]
                        for lst, whole in ((raB, ra.b), (guB, gu.b), (t1B, t1.b)):
                            for bb in lst:
                                bb.w = whole.w
                                bb.r = dict(whole.r)
                        for tt in order:
                            tsl = slice(tt * 512, (tt + 1) * 512)
                            pR = self.ps[4 + tt % 2]
                            pI = self.ps[6 + tt % 2]
                            self.op(self.pe, lambda: nc.tensor.matmul(pR.ap, lhsT=bd_.t[:, dr_, :], rhs=xcb.t[:, tsl],
                                                                      start=True, stop=True),
                                    reads=[bdB[cc % 2][dr_], xcb.b], writes=[pR.b])
                            self.op(self.pe, lambda: nc.tensor.matmul(pI.ap, lhsT=bd_.t[:, 2 + dr_, :], rhs=xcb.t[:, tsl],
                                                                      start=True, stop=True),
                                    reads=[bdB[cc % 2][2 + dr_], xcb.b], writes=[pI.b])
                            ba = self.pcol.t[:, PC["rba"] + dr_ * 4 + cc: PC["rba"] + dr_ * 4 + cc + 1]
                            bx = self.pcol.t[:, PC["rbx"] + dr_ * 4 + cc: PC["rbx"] + dr_ * 4 + cc + 1]
                            self.op(self.act, lambda: nc.scalar.activation(out=ra.t[:, tsl], in_=pR.ap, func=AF.Sigmoid,
                                                                           bias=ba, scale=1.0),
                                    reads=[pR.b, self.pcol.b], writes=[raB[tt]])
                            self.op(self.act, lambda: nc.scalar.activation(out=gu.t[:, tsl], in_=pI.ap, func=AF.Sigmoid,
                                                                           bias=bx, scale=1.0),
                                    reads=[pI.b, self.pcol.b], writes=[guB[tt]])
                        cc_ = ccol.t[:, dr_ * 4 + cc: dr_ * 4 + cc + 1]
                        hsd = hs[dr_]
                        hsB = [Buf() for _ in range(4)]
                        for bb in hsB:
                            bb.w = hsd.b.w
                            bb.r = dict(hsd.b.r)
                        prev = None
                        for tt in order:
                            tsl = slice(tt * 512, (tt + 1) * 512)
                            self.op(self.act, lambda: nc.scalar.activation(out=ra.t[:, tsl], in_=ra.t[:, tsl], func=AF.Exp, scale=cc_),
                                    reads=[raB[tt], ccol.b], writes=[raB[tt]])
                            self.op(self.dve, lambda: nc.vector.tensor_tensor(out=t1.t[:, tsl], in0=ra.t[:, tsl], in1=ra.t[:, tsl], op=ALU.mult),
                                    reads=[raB[tt]], writes=[t1B[tt]])
                            self.op(self.act, lambda: nc.scalar.activation(out=t1.t[:, tsl], in_=t1.t[:, tsl], func=AF.Ln,
                                                                           bias=self.one.t[:, 0:1], scale=-1.0),
                                    reads=[t1B[tt], self.one.b], writes=[t1B[tt]])
                            self.op(self.act, lambda: nc.scalar.activation(out=t1.t[:, tsl], in_=t1.t[:, tsl], func=AF.Exp, scale=0.5),
                                    reads=[t1B[tt]], writes=[t1B[tt]])
                            self.op(self.dve, lambda: nc.vector.tensor_tensor(out=gu.t[:, tsl], in0=gu.t[:, tsl], in1=xc.t[:, tsl], op=ALU.mult),
                                    reads=[guB[tt], xc.b], writes=[guB[tt]])
                            self.op(self.dve, lambda: nc.vector.tensor_tensor(out=gu.t[:, tsl], in0=gu.t[:, tsl], in1=t1.t[:, tsl], op=ALU.mult),
                                    reads=[guB[tt], t1B[tt]], writes=[guB[tt]])
                            if dr_ == 0:
                                init = 0.0 if prev is None else hsd.t[:, tt * 512 - 1: tt * 512]
                                self.op(self.dve, lambda: nc.vector.tensor_tensor_scan(
                                    out=hsd.t[:, tsl], data0=ra.t[:, tsl], data1=gu.t[:, tsl], initial=init,
                                    op0=ALU.mult, op1=ALU.add),
                                    reads=[raB[tt], guB[tt]] + ([hsB[prev]] if prev is not None else []), writes=[hsB[tt]])
                            else:
                                init = 0.0 if prev is None else hsd.t[:, (tt + 1) * 512: (tt + 1) * 512 + 1]
                                rsl = slice((tt + 1) * 512 - 1, tt * 512 - 1 if tt > 0 else None, -1)
                                self.op(self.dve, lambda: nc.vector.tensor_tensor_scan(
                                    out=hsd.t[:, rsl], data0=ra.t[:, rsl], data1=gu.t[:, rsl], initial=init,
                                    op0=ALU.mult, op1=ALU.add),
                                    reads=[raB[tt], guB[tt]] + ([hsB[prev]] if prev is not None else []), writes=[hsB[tt]])
                            prev = tt
                        def fold(whole, lst):
                            r = {}
                            w = None
                            for bb in lst:
                                if bb.w is not None:
                                    assert w is None or w[0] is bb.w[0]
                                    if w is None or bb.w[1] > w[1]:
                                        w = bb.w
                                for sm, v in bb.r.items():
                                    r[sm] = max(r.get(sm, 0), v)
                            whole.w = w
                            whole.r = r
                        for whole, lst in ((ra.b, raB), (gu.b, guB), (t1.b, t1B), (hsd.b, hsB)):
                            fold(whole, lst)
                    self.op(self.dve, lambda: nc.vector.tensor_tensor(out=hs[0].t[:], in0=hs[0].t[:], in1=hs[1].t[:], op=ALU.add),
                            reads=[hs[0].b, hs[1].b], writes=[hs[0].b])
                    self.op(self.dve, lambda: nc.vector.tensor_tensor(out=yc.t[:, cc, :], in0=hs[0].t[:], in1=gg.t[:], op=ALU.mult),
                            reads=[hs[0].b, gg.b], writes=ycB[cc])
                self.barrier()
            self.epilogue(l, 2, ph, yc, ycB, "w_proj_rg")

    def branch_attn(self, l):
        nc = self.nc
        win = self.w["w_in"][l].rearrange("(kc p) c -> p kc c", p=128)
        with ExitStack() as ph:
            yb = self.sb(ph, [128, 4, T], BF16)
            ybB = [[Buf() for _ in range(4)] for _ in range(4)]
            with ExitStack() as p1:
                geo = []
                for (win_, d) in GROUPS:
                    Ls = T // d
                    nb = Ls // 128
                    off = 64 if nb > 1 else 0
                    nch = nb + (1 if off else 0)
                    geo.append((d, Ls, nb, off, nch))
                QT = [self.sb(p1, [128, T], BF16) for _ in range(3)]
                KT = [self.sb(p1, [128, geo[g][0] * (geo[g][1] + 2 * geo[g][3])], BF16) for g in range(3)]
                VT = [self.sb(p1, [128, geo[g][0] * geo[g][4], 128], BF16) for g in range(3)]
                Eb = [self.sb(p1, [128, 8, 256 if geo[g_][3] else 128], BF16) for g_ in range(3)]
                with ExitStack() as pe_:
                    est = self.sb(pe_, [128, 8, 256], F32)
                    for g in range(3):
                        self.dma(self.sp, est.t[:], self.d_ab[g], writes=[est.b])
                        ew = 256 if geo[g][3] else 128
                        self.op(self.act, lambda: nc.scalar.activation(out=Eb[g].t[:], in_=est.t[:, :, 0:ew], func=AF.Exp),
                                reads=[est.b], writes=[Eb[g].b])
                    self.barrier()
                wq = [self.sb(p1, [128, 8, 128], BF16) for _ in range(2)]
                wk = [self.sb(p1, [128, 8, 128], BF16) for _ in range(2)]
                wv = [self.sb(p1, [128, 8, 128], BF16) for _ in range(2)]
                tot = [self.sb(p1, [128, T], F32) for _ in range(2)]
                rec = self.sb(p1, [64, T], F32)
                pt = [self.sb(p1, [128, 256], BF16) for _ in range(3)]
                psSB = [Buf() for _ in range(4)]
                for g in range(3):
                    self.op(self.dve, lambda: nc.vector.memset(KT[g].t[:], 0.0), writes=[KT[g].b])
                    self.op(self.dve, lambda: nc.vector.memset(VT[g].t[:], 0.0), writes=[VT[g].b])
                li = 0
                loads = [(hp, g) for hp in range(4) for g in range(3)]

                def load(i):
                    hp, g = loads[i]
                    for qi, wt in enumerate((wq, wk, wv)):
                        c0 = 1536 + ((qi * 3 + g) * 8) * 64 + hp * 128
                        self.dma(self.pool, wt[i % 2].t[:], win[:, :, c0:c0 + 128], writes=[wt[i % 2].b])

                load(0)
                pcnt = [0]
                for hp in range(4):
                    for g in range(3):
                        i = hp * 3 + g
                        if i + 1 < len(loads):
                            load(i + 1)
                        d, Ls, nb, off, nch = geo[g]
                        wq_, wk_, wv_ = wq[i % 2], wk[i % 2], wv[i % 2]
                        QTv = QT[g].t[:].rearrange("p (r j) -> p r j", r=d)
                        KTv = KT[g].t[:].rearrange("p (r j) -> p r j", r=d)
                        for tt in range(4):
                            tsl = slice(tt * 512, (tt + 1) * 512)
                            for which, (w_, dstv, poff) in enumerate(((wq_, QTv, 0), (wk_, KTv, off))):
                                pp = self.ps[6 + (tt * 2 + which) % 2]
                                for kc in range(8):
                                    self.op(self.pe, lambda: nc.tensor.matmul(pp.ap, lhsT=w_.t[:, kc, :],
                                                                              rhs=self.xn.t[:, kc, tsl],
                                                                              start=(kc == 0), stop=(kc == 7)),
                                            reads=[w_.b, self.xnB[kc][tt]], writes=[pp.b], inc=(kc == 7))
                                nj = 512 // d
                                j0 = tt * nj
                                dst = dstv[:, :, poff + j0: poff + j0 + nj]
                                src = pp.ap.rearrange("p (j r) -> p r j", r=d)
                                dB = QT[g].b if which == 0 else KT[g].b
                                if which == 0:
                                    self.op(self.act, lambda: nc.scalar.activation(out=dst, in_=src, func=AF.Copy),
                                            reads=[pp.b], writes=[dB])
                                else:
                                    self.op(self.dve, lambda: nc.vector.tensor_copy(out=dst, in_=src),
                                            reads=[pp.b], writes=[dB])
                        vcnt = 0
                        for r in range(d):
                            for c in range(nch):
                                jlo = max(0, 128 * c - off)
                                jhi = min(Ls, 128 * c - off + 128)
                                kk0 = jlo - (128 * c - off)
                                nr = jhi - jlo
                                ci = r * nch + c
                                pv = self.ps[6 + vcnt % 2]
                                vcnt += 1
                                t0 = r + d * jlo
                                for kc in range(8):
                                    lhsT = self.xn.t[:, kc, t0: t0 + d * (nr - 1) + 1: d]
                                    self.op(self.pe, lambda: nc.tensor.matmul(
                                        pv.ap[kk0:kk0 + nr, 0:128], lhsT=lhsT, rhs=wv_.t[:, kc, :],
                                        start=(kc == 0), stop=(kc == 7)),
                                        reads=[wv_.b] + [self.xnB[kc][q] for q in range(4)], writes=[pv.b], inc=(kc == 7))
                                if vcnt % 2:
                                    self.op(self.act, lambda: nc.scalar.activation(
                                        out=VT[g].t[kk0:kk0 + nr, ci, :], in_=pv.ap[kk0:kk0 + nr, 0:128], func=AF.Copy),
                                        reads=[pv.b], writes=[VT[g].b])
                                else:
                                    self.op(self.dve, lambda: nc.vector.tensor_copy(
                                        out=VT[g].t[kk0:kk0 + nr, ci, :], in_=pv.ap[kk0:kk0 + nr, 0:128]),
                                        reads=[pv.b], writes=[VT[g].b])
                        tiles = [(hh, r, c) for hh in range(2) for r in range(d) for c in range(nch)]
                        LA = 2
                        started = [set(), set()]
                        info = {}

                        def front(k):
                            hh, r, c = tiles[k]
                            hd = 2 * hp + hh
                            prt = slice(hh * 64, (hh + 1) * 64)
                            blocks = [bi for bi in ((c - 1, c) if off else (c,)) if 0 <= bi < nb]
                            qa = 128 * blocks[0]
                            nq = 128 * len(blocks)
                            ecol0 = qa - 128 * (c - 1) if off else 0
                            kq = pcnt[0] % 4
                            pS_ap = self.ps[4 + kq].ap[:, 0:nq]
                            pSb = self.ps[4 + kq].b
                            p_ = pt[pcnt[0] % len(pt)]
                            pcnt[0] += 1
                            info[k] = (p_, blocks)
                            self.op(self.pe, lambda: nc.tensor.matmul(
                                pS_ap, lhsT=KTv[prt, r, 128 * c: 128 * c + 128],
                                rhs=QTv[prt, r, qa: qa + nq], start=True, stop=True),
                                reads=[KT[g].b, QT[g].b], writes=[pSb])
                            self.op(self.act, lambda: nc.scalar.activation(out=p_.t[:, 0:nq], in_=pS_ap,
                                                                           func=AF.Exp, scale=0.125),
                                    reads=[pSb], writes=[p_.b])
                            self.op(self.dve, lambda: nc.vector.tensor_tensor(
                                out=p_.t[:, 0:nq], in0=p_.t[:, 0:nq], in1=Eb[g].t[:, hd, ecol0: ecol0 + nq],
                                op=ALU.mult), reads=[p_.b, Eb[g].b], writes=[p_.b])

                        def back(k):
                            hh, r, c = tiles[k]
                            prt = slice(hh * 64, (hh + 1) * 64)
                            ci = r * nch + c
                            p_, blocks = info.pop(k)
                            if off:
                                ov = 0 if c == 0 else (2 if c == nb else 1)
                            else:
                                ov = 1
                            for bi, blk in enumerate(blocks):
                                col0 = r * Ls + 128 * blk
                                bank = col0 // 512
                                rhs = p_.t[:, bi * 128:(bi + 1) * 128]
                                s1 = ("n", bank) not in started[hh]
                                started[hh].add(("n", bank))
                                self.op(self.pe, lambda: nc.tensor.matmul(
                                    self.acc.t[0:64, col0: col0 + 128], lhsT=VT[g].t[:, ci, prt], rhs=rhs,
                                    start=s1, stop=True, skip_group_check=True),
                                    reads=[VT[g].b, p_.b], writes=[self.acc.b], inc=False)
                                s2 = ("d", bank) not in started[hh]
                                started[hh].add(("d", bank))
                                self.op(self.pe, lambda: nc.tensor.matmul(
                                    self.acc.t[64:128, col0: col0 + 128], lhsT=self.onesb.t[:, ov, :], rhs=rhs,
                                    start=s2, stop=True, skip_group_check=True),
                                    reads=[self.onesb.b, p_.b], writes=[self.acc.b], inc=True)
                            if r == d - 1 and c == nch - 1:
                                tt_ = tot[hh]
                                if g == 0:
                                    self.op(self.act, lambda: nc.scalar.activation(out=tt_.t[:], in_=self.acc.t[:], func=AF.Copy),
                                            reads=[self.acc.b], writes=[tt_.b])
                                else:
                                    tv = tt_.t[:].rearrange("p (j r) -> p r j", r=d)
                                    av = self.acc.t[:].rearrange("p (r j) -> p r j", r=d)
                                    self.op(self.dve, lambda: nc.vector.tensor_tensor(out=tv, in0=av, in1=tv, op=ALU.add),
                                            reads=[self.acc.b, tt_.b], writes=[tt_.b])

                        for k in range(len(tiles) + LA):
                            if k < len(tiles):
                                front(k)
                            if k >= LA:
                                back(k - LA)
                    for hh in range(2):
                        tt_ = tot[hh]
                        self.op(self.act, lambda: nc.scalar.activation(out=rec.t[0:64, :], in_=tt_.t[64:128, :], func=AF.Ln),
                                reads=[tt_.b], writes=[rec.b])
                        self.op(self.act, lambda: nc.scalar.activation(out=rec.t[0:64, :], in_=rec.t[0:64, :], func=AF.Exp, scale=-1.0),
                                reads=[rec.b], writes=[rec.b])
                        self.op(self.dve, lambda: nc.vector.tensor_tensor(
                            out=yb.t[hh * 64:(hh + 1) * 64, hp, :], in0=tt_.t[0:64, :], in1=rec.t[0:64, :], op=ALU.mult),
                            reads=[tt_.b, rec.b], writes=ybB[hp])
                self.barrier()
            self.epilogue(l, 1, ph, yb, ybB, "w_proj_attn")

    def branch_hy(self, l):
        nc = self.nc
        win = self.w["w_in"][l].rearrange("(kc p) c -> p kc c", p=128)
        with ExitStack() as ph:
            ya = self.sb(ph, [128, 4, T], BF16)
            yaB = [[Buf() for _ in range(4)] for _ in range(4)]
            with ExitStack() as p1:
                rall = self.sb(p1, [128, T], F32)
                self.rmsnorm(self.pcol.t[:, PC["nm"]:PC["nm"] + 8], save=rall)
                hdn3 = self.sb(p1, [64, T], BF16)
                with ExitStack() as p0:
                    zt = self.sb(p0, [33, T], F32)
                    w1 = self.sb(p0, [33, 64], F32)
                    w2 = self.sb(p0, [64, 64], F32)
                    w3 = self.sb(p0, [64, 64], F32)
                    ha = self.sb(p0, [64, T], F32)
                    hb_ = self.sb(p0, [64, T], F32)
                    s_ = [self.sb(p0, [64, 512], F32) for _ in range(2)]
                    s2 = [self.sb(p0, [64, 512], F32) for _ in range(2)]
                    cols = self.sb(p0, [64, 4], F32)
                    self.dma(self.sp, zt.t[:], self.d_zT[:, :], writes=[zt.b])
                    self.dma(self.sp, w1.t[:], self.d_w1[l], writes=[w1.b])
                    self.dma(self.sp, w2.t[:], self.d_w2[l], writes=[w2.b])
                    self.dma(self.sp, w3.t[:], self.d_w3[l], writes=[w3.b])
                    fr = self.pcol.t[0:64, PC["hfr"]:PC["hfr"] + 1]
                    self.op(self.dve, lambda: nc.vector.tensor_scalar(out=cols.t[:, 0:1], in0=fr, scalar1=1.0 / 3.0,
                                                                      scalar2=None, op0=ALU.mult),
                            reads=[self.pcol.b], writes=[cols.b])
                    for k in range(3):
                        bk = self.pcol.t[0:64, PC["hb1"] + k: PC["hb1"] + k + 1]
                        self.op(self.dve, lambda: nc.vector.tensor_tensor(out=cols.t[:, k + 1:k + 2], in0=bk,
                                                                          in1=cols.t[:, 0:1], op=ALU.mult),
                                reads=[self.pcol.b, cols.b], writes=[cols.b])
                    srcs = [(zt, w1, 33), (ha, w2, 64), (hb_, w3, 64)]
                    dsts = [ha, hb_, hdn3]
                    for k in range(3):
                        src, wk_, kk = srcs[k]
                        dst = dsts[k]
                        for tt in range(4):
                            tsl = slice(tt * 512, (tt + 1) * 512)
                            pm = self.ps[tt % 2]
                            sa, sb2 = s_[tt % 2], s2[tt % 2]
                            self.op(self.pe, lambda: nc.tensor.matmul(pm.ap[0:64, :], lhsT=wk_.t[0:kk, :], rhs=src.t[0:kk, tsl],
                                                                      start=True, stop=True),
                                    reads=[wk_.b, src.b], writes=[pm.b])
                            self.op(self.act, lambda: nc.scalar.activation(out=sa.t[:], in_=pm.ap[0:64, :], func=AF.Sin,
                                                                           bias=cols.t[:, k + 1:k + 2], scale=cols.t[:, 0:1]),
                                    reads=[pm.b, cols.b], writes=[sa.b])
                            self.op(self.dve, lambda: nc.vector.tensor_tensor(out=sb2.t[:], in0=sa.t[:], in1=sa.t[:], op=ALU.mult),
                                    reads=[sa.b], writes=[sb2.b])
                            self.op(self.dve, lambda: nc.vector.tensor_scalar(out=sb2.t[:], in0=sb2.t[:], scalar1=-4.0,
                                                                              scalar2=3.0, op0=ALU.mult, op1=ALU.add),
                                    reads=[sb2.b], writes=[sb2.b])
                            self.op(self.dve, lambda: nc.vector.tensor_tensor(out=dst.t[:, tsl], in0=sb2.t[:], in1=sa.t[:], op=ALU.mult),
                                    reads=[sb2.b, sa.b], writes=[dst.b])
                    self.barrier()
                for hf in range(2):
                    if hf == 1:
                        self.rmsnorm(self.pcol.t[:, PC["nm"]:PC["nm"] + 8], load=rall)
                    self.hy_half(l, hf, p1, hdn3, ya, yaB, win)
                self.rmsnorm(self.pcol.t[:, PC["nm"]:PC["nm"] + 8], load=rall)
                self.barrier()
            self.epilogue(l, 0, ph, ya, yaB, "w_proj_hy")

    def hy_half(self, l, hf, p1, hdn3, ya, yaB, win):
        nc = self.nc
        with ExitStack() as hh:
            ZK = self.sb(hh, [128, 16, 3, 256], BF16)
            zB = [Buf() for _ in range(16)]
            kB = Buf()
            x1h = self.sb(hh, [128, 2, T], BF16)
            x2h = self.sb(hh, [128, 2, T], BF16)
            with ExitStack() as pin:
                wsl = [self.sb(pin, [128, 8, 128], BF16) for _ in range(2)]
                upads = [self.sb(pin, [128, T + 2], F32) for _ in range(2)]
                uc = self.sb(pin, [128, T], F32)
                vfm = self.sb(pin, [128, T], BF16)
                for upad in upads:
                    self.op(self.dve, lambda: nc.vector.memset(upad.t[:, 0:1], 0.0), writes=[upad.b])
                    self.op(self.dve, lambda: nc.vector.memset(upad.t[:, T + 1:T + 2], 0.0), writes=[upad.b])
                items = [(part, cch) for part in range(3) for cch in range(2)]

                def load(i):
                    part, cch = items[i]
                    c0 = part * 512 + hf * 256 + cch * 128
                    self.dma(self.pool, wsl[i % 2].t[:], win[:, :, c0:c0 + 128], writes=[wsl[i % 2].b])

                load(0)
                def inproj(i):
                    part, cch = items[i]
                    if i + 1 < len(items):
                        load(i + 1)
                    w_ = wsl[i % 2]
                    upad = upads[i % 2]
                    for tt in range(4):
                        tsl = slice(tt * 512, (tt + 1) * 512)
                        pp = self.ps[tt % 4]
                        for kc in range(8):
                            self.op(self.pe, lambda: nc.tensor.matmul(pp.ap, lhsT=w_.t[:, kc, :], rhs=self.xn.t[:, kc, tsl],
                                                                      start=(kc == 0), stop=(kc == 7)),
                                    reads=[w_.b, self.xnB[kc][tt]], writes=[pp.b], inc=(kc == 7))
                        self.op(self.act, lambda: nc.scalar.activation(out=upad.t[:, 1 + tt * 512: 1 + (tt + 1) * 512],
                                                                       in_=pp.ap, func=AF.Copy),
                                reads=[pp.b], writes=[upad.b])

                def convpart(i):
                    part, cch = items[i]
                    upad = upads[i % 2]
                    gch = part * 4 + hf * 2 + cch
                    cw = lambda k: self.pcol.t[:, PC["hcw"] + k * 12 + gch: PC["hcw"] + k * 12 + gch + 1]
                    cb = self.pcol.t[:, PC["hcb"] + gch: PC["hcb"] + gch + 1]
                    self.op(self.dve, lambda: nc.vector.tensor_scalar(out=uc.t[:], in0=upad.t[:, 0:T], scalar1=cw(0),
                                                                      scalar2=cb, op0=ALU.mult, op1=ALU.add),
                            reads=[upad.b, self.pcol.b], writes=[uc.b])
                    self.op(self.dve, lambda: nc.vector.scalar_tensor_tensor(
                        out=uc.t[:], in0=upad.t[:, 1:1 + T], scalar=cw(1), in1=uc.t[:], op0=ALU.mult, op1=ALU.add),
                        reads=[upad.b, uc.b, self.pcol.b], writes=[uc.b])
                    if part == 0:
                        dst, dB = vfm.t[:], [vfm.b]
                    elif part == 1:
                        dst, dB = x1h.t[:, cch, :], [x1h.b]
                    else:
                        dst, dB = x2h.t[:, cch, :], [x2h.b]
                    self.op(self.dve, lambda: nc.vector.scalar_tensor_tensor(
                        out=dst, in0=upad.t[:, 2:2 + T], scalar=cw(2), in1=uc.t[:], op0=ALU.mult, op1=ALU.add),
                        reads=[upad.b, uc.b, self.pcol.b], writes=dB)
                    if part == 0:
                        for q in range(4):
                            pT = self.ps[4 + q % 2]
                            pTb = pT.ap.bitcast(BF16)
                            for k4 in range(4):
                                tc_ = q * 4 + k4
                                self.op(self.pe, lambda: nc.tensor.transpose(
                                    pTb[:, k4 * 128:(k4 + 1) * 128], vfm.t[:, tc_ * 128:(tc_ + 1) * 128], self.ident.t[:]),
                                    reads=[vfm.b, self.ident.b], writes=[pT.b], inc=(k4 == 3))
                            self.op(self.act, lambda: nc.scalar.activation(
                                out=ZK.t[:, q * 4:(q + 1) * 4, 0, cch * 128:(cch + 1) * 128],
                                in_=pTb[:, 0:512].rearrange("p (k c) -> p k c", k=4), func=AF.Copy),
                                reads=[pT.b], writes=zB[q * 4:(q + 1) * 4])

                for i in range(len(items) + 1):
                    if i < len(items):
                        inproj(i)
                    if i >= 1:
                        convpart(i - 1)
                self.barrier()
            with ExitStack() as pm:
                xflat = self.xn.t[:].rearrange("p a b -> p (a b)")
                Y = Tile(None)
                Y.t = xflat[:, 0:NF * 512].rearrange("p (f r c) -> p f r c", f=NF, r=2)
                YB = [Buf() for _ in range(NF)]
                wo_ = self.sb(pm, [64, 2, 256], BF16)
                woB2 = [Buf(), Buf()]
                dl = self.sb(pm, [128, 256], F32)
                tcl = self.sb(pm, [128, 16], F32)
                skb = self.sb(pm, [128, 256], F32)
                rn = self.sb(pm, [128, 256], F32)
                dec = [self.sb(pm, [128, 256], F32) for _ in range(3)]
                kf = [self.sb(pm, [128, 256], F32) for _ in range(3)]
                kb = [self.sb(pm, [128, 256], F32) for _ in range(3)]
                ab = [self.sb(pm, [128, 256], F32) for _ in range(3)]
                abacc = self.sb(pm, [128, 256], F32)
                kr, ki, ta, tb = dec, kf, kb, ab
                fsl = [self.sb(pm, [128, 2, 16, 128], BF16) for _ in range(2)]
                isl = []
                for i_ in range(3):
                    tl_ = Tile(None)
                    tl_.t = xflat[:, NF * 512 + i_ * 2048: NF * 512 + (i_ + 1) * 2048].rearrange("p (r c) -> p r c", r=2)
                    isl.append(tl_)
                zfm = []
                for i_ in range(2):
                    tl_ = Tile(None)
                    z0 = NF * 512 + 3 * 2048 + i_ * 512
                    tl_.t = xflat[:, z0:z0 + 512]
                    zfm.append(tl_)
                self.dma(self.sp, dl.t[:], self.d_delta[:, hf * 256:(hf + 1) * 256], writes=[dl.b])
                self.dma(self.sp, tcl.t[:], self.d_tcol[:, :], writes=[tcl.b])
                for o in range(2):
                    for dr_ in range(2):
                        c0 = dr_ * 1024 + o * 512 + hf * 256
                        self.dma(self.pool, wo_.t[:, dr_, :], self.d_wout[l, :, c0:c0 + 256], writes=[woB2[dr_]])
                    self.dma(self.sp, skb.t[:], self.d_skip[l, o, :, hf * 256:(hf + 1) * 256], writes=[skb.b])
                    pN = self.ps[7]

                    def stA(tc_):
                        i3 = tc_ % 3
                        pK = self.ps[i3]
                        self.op(self.pe, lambda: nc.tensor.matmul(pK.ap, lhsT=hdn3.t[0:64, tc_ * 128:(tc_ + 1) * 128],
                                                                  rhs=wo_.t[0:64, :, :], start=True, stop=True),
                                reads=[hdn3.b, woB2[0], woB2[1]], writes=[pK.b])
                        self.op(self.act, lambda: nc.scalar.activation(out=dec[i3].t[:], in_=dl.t[:], func=AF.Exp,
                                                                       scale=tcl.t[:, tc_:tc_ + 1]),
                                reads=[dl.b, tcl.b], writes=[dec[i3].b])

                    def stB(tc_):
                        i3 = tc_ % 3
                        pK = self.ps[i3]
                        self.op(self.dve, lambda: nc.vector.scalar_tensor_tensor(
                            out=kf[i3].t[:], in0=dec[i3].t[:], scalar=0.05, in1=pK.ap[:, 0:256], op0=ALU.add, op1=ALU.mult),
                            reads=[dec[i3].b, pK.b], writes=[kf[i3].b])
                        self.op(self.dve, lambda: nc.vector.scalar_tensor_tensor(
                            out=kb[i3].t[:], in0=dec[i3].t[:], scalar=0.05, in1=pK.ap[:, 256:512], op0=ALU.add, op1=ALU.mult),
                            reads=[dec[i3].b, pK.b], writes=[kb[i3].b])
                        if tc_ == 0:
                            self.op(self.dve, lambda: nc.vector.memset(kb[i3].t[0:1, :], 0.0), reads=[kb[i3].b], writes=[kb[i3].b])

                    def stC(tc_):
                        i3 = tc_ % 3
                        self.op(self.pool, lambda: nc.gpsimd.tensor_tensor(out=ZK.t[:, tc_, 1, :], in0=kf[i3].t[:], in1=kb[i3].t[:], op=ALU.add),
                                reads=[kf[i3].b, kb[i3].b], writes=[kB])
                        self.op(self.pool, lambda: nc.gpsimd.tensor_tensor(out=ZK.t[:, tc_, 2, :], in0=kf[i3].t[:], in1=kb[i3].t[:], op=ALU.subtract),
                                reads=[kf[i3].b, kb[i3].b], writes=[kB])
                        self.op(self.act, lambda: nc.scalar.activation(out=ab[i3].t[:], in_=kf[i3].t[:], func=AF.Abs),
                                reads=[kf[i3].b], writes=[ab[i3].b])
                        self.op(self.act, lambda: nc.scalar.activation(out=dec[i3].t[:], in_=kb[i3].t[:], func=AF.Abs),
                                reads=[kb[i3].b], writes=[dec[i3].b])
                        if tc_ == 0:
                            self.op(self.dve, lambda: nc.vector.tensor_tensor(out=abacc.t[:], in0=ab[i3].t[:], in1=dec[i3].t[:], op=ALU.add),
                                    reads=[ab[i3].b, dec[i3].b], writes=[abacc.b])
                        else:
                            self.op(self.dve, lambda: nc.vector.tensor_tensor(out=ab[i3].t[:], in0=ab[i3].t[:], in1=dec[i3].t[:], op=ALU.add),
                                    reads=[ab[i3].b, dec[i3].b], writes=[ab[i3].b])
                            self.op(self.pool, lambda: nc.gpsimd.tensor_tensor(out=abacc.t[:], in0=abacc.t[:], in1=ab[i3].t[:], op=ALU.add),
                                    reads=[abacc.b, ab[i3].b], writes=[abacc.b])

                    for st_ in range(16 + 2):
                        if st_ < 16:
                            stA(st_)
                        if 0 <= st_ - 1 < 16:
                            stB(st_ - 1)
                        if 0 <= st_ - 2 < 16:
                            stC(st_ - 2)
                    self.op(self.pe, lambda: nc.tensor.matmul(pN.ap[:, 0:256], lhsT=self.ones32.t[:], rhs=abacc.t[:],
                                                              start=True, stop=True),
                            reads=[abacc.b, self.ones32.b], writes=[pN.b])
                    self.op(self.dve, lambda: nc.vector.tensor_scalar(out=rn.t[:], in0=pN.ap[:, 0:256], scalar1=1e-6,
                                                                      scalar2=None, op0=ALU.add),
                            reads=[pN.b], writes=[rn.b])
                    self.op(self.dve, lambda: nc.vector.reciprocal(out=rn.t[:], in_=rn.t[:]), reads=[rn.b], writes=[rn.b])
                    self.dma(self.sp, fsl[0].t[:], self.d_csf[0], writes=[fsl[0].b])
                    for fc in range(NF):
                        if fc + 1 < NF:
                            self.dma(self.sp, fsl[(fc + 1) % 2].t[:], self.d_csf[fc + 1], writes=[fsl[(fc + 1) % 2].b])
                        fs_ = fsl[fc % 2]
                        pA = self.ps[(2 * fc) % 4]
                        pB = self.ps[(2 * fc + 1) % 4]
                        for tc_ in range(16):
                            self.op(self.pe, lambda: nc.tensor.matmul(pA.ap, lhsT=fs_.t[:, 0, tc_, :], rhs=ZK.t[:, tc_, 0:2, :],
                                                                      start=(tc_ == 0), stop=(tc_ == 15)),
                                    reads=[fs_.b, zB[tc_], kB], writes=[pA.b], inc=(tc_ == 15))
                        for tc_ in range(16):
                            self.op(self.pe, lambda: nc.tensor.matmul(pB.ap, lhsT=fs_.t[:, 1, tc_, :], rhs=ZK.t[:, tc_, 0:3:2, :],
                                                                      start=(tc_ == 0), stop=(tc_ == 15)),
                                    reads=[fs_.b, zB[tc_], kB], writes=[pB.b], inc=(tc_ == 15))
                        i2 = fc % 2
                        V = nc.vector
                        self.op(self.dve, lambda: V.tensor_tensor(out=kr[i2].t[:], in0=pA.ap[:, 256:512], in1=rn.t[:], op=ALU.mult),
                                reads=[pA.b, rn.b], writes=[kr[i2].b])
                        self.op(self.dve, lambda: V.tensor_tensor(out=kr[i2].t[:], in0=kr[i2].t[:], in1=skb.t[:], op=ALU.add),
                                reads=[kr[i2].b, skb.b], writes=[kr[i2].b])
                        self.op(self.dve, lambda: V.tensor_tensor(out=ki[i2].t[:], in0=pB.ap[:, 256:512], in1=rn.t[:], op=ALU.mult),
                                reads=[pB.b, rn.b], writes=[ki[i2].b])
                        self.op(self.dve, lambda: V.tensor_tensor(out=ta[i2].t[:], in0=pA.ap[:, 0:256], in1=kr[i2].t[:], op=ALU.mult),
                                reads=[pA.b, kr[i2].b], writes=[ta[i2].b])
                        self.op(self.dve, lambda: V.tensor_tensor(out=tb[i2].t[:], in0=pB.ap[:, 0:256], in1=ki[i2].t[:], op=ALU.mult),
                                reads=[pB.b, ki[i2].b], writes=[tb[i2].b])
                        self.op(self.dve, lambda: V.tensor_tensor(out=Y.t[:, fc, 0, :], in0=ta[i2].t[:], in1=tb[i2].t[:], op=ALU.subtract),
                                reads=[ta[i2].b, tb[i2].b], writes=[YB[fc]])
                        self.op(self.dve, lambda: V.tensor_tensor(out=ta[i2].t[:], in0=pA.ap[:, 0:256], in1=ki[i2].t[:], op=ALU.mult),
                                reads=[pA.b, ki[i2].b, ta[i2].b], writes=[ta[i2].b])
                        self.op(self.dve, lambda: V.tensor_tensor(out=tb[i2].t[:], in0=pB.ap[:, 0:256], in1=kr[i2].t[:], op=ALU.mult),
                                reads=[pB.b, kr[i2].b, tb[i2].b], writes=[tb[i2].b])
                        self.op(self.dve, lambda: V.tensor_tensor(out=Y.t[:, fc, 1, :], in0=ta[i2].t[:], in1=tb[i2].t[:], op=ALU.add),
                                reads=[ta[i2].b, tb[i2].b], writes=[YB[fc]])
                    icnt = 0
                    for th in range(2):
                        seq = [(fc) for fc in range(NF)]
                        self.dma(self.sp, isl[icnt % 3].t[:], self.d_csi[0, th],
                                 writes=[isl[icnt % 3].b])
                        for fc in range(NF):
                            if fc + 1 < NF:
                                self.dma(self.sp, isl[(icnt + 1) % 3].t[:], self.d_csi[fc + 1, th],
                                         writes=[isl[(icnt + 1) % 3].b])
                            is_ = isl[icnt % 3]
                            icnt += 1
                            for cch in range(2):
                                for t2 in range(2):
                                    pO = self.ps[cch * 2 + t2]
                                    for ri in range(2):
                                        last = (fc == NF - 1 and ri == 1)
                                        self.op(self.pe, lambda: nc.tensor.matmul(
                                            pO.ap, lhsT=Y.t[:, fc, ri, cch * 128:(cch + 1) * 128],
                                            rhs=is_.t[:, ri, t2 * 512:(t2 + 1) * 512],
                                            start=(fc == 0 and ri == 0), stop=last),
                                            reads=[YB[fc], is_.b], writes=[pO.b], inc=(last or (cch == 1 and t2 == 1 and ri == 1)))
                        for cch in range(2):
                            for t2 in range(2):
                                pO = self.ps[cch * 2 + t2]
                                tt = th * 2 + t2
                                tsl = slice(tt * 512, (tt + 1) * 512)
                                if o == 0:
                                    zf = zfm[(cch * 2 + t2) % 2]
                                    self.op(self.dve, lambda: nc.vector.tensor_tensor(out=zf.t[:], in0=pO.ap, in1=x1h.t[:, cch, tsl], op=ALU.mult),
                                            reads=[pO.b, x1h.b], writes=[zf.b])
                                    pT = self.ps[4 + (cch * 2 + t2) % 2]
                                    pTb = pT.ap.bitcast(BF16)
                                    for k4 in range(4):
                                        self.op(self.pe, lambda: nc.tensor.transpose(
                                            pTb[:, k4 * 128:(k4 + 1) * 128], zf.t[:, k4 * 128:(k4 + 1) * 128], self.ident.t[:]),
                                            reads=[zf.b, self.ident.b], writes=[pT.b], inc=(k4 == 3))
                                    self.op(self.act, lambda: nc.scalar.activation(
                                        out=ZK.t[:, tt * 4:(tt + 1) * 4, 0, cch * 128:(cch + 1) * 128],
                                        in_=pTb[:, 0:512].rearrange("p (k c) -> p k c", k=4), func=AF.Copy),
                                        reads=[pT.b], writes=zB[tt * 4:(tt + 1) * 4])
                                else:
                                    self.op(self.dve, lambda: nc.vector.tensor_tensor(
                                        out=ya.t[:, hf * 2 + cch, tsl], in0=pO.ap, in1=x2h.t[:, cch, tsl], op=ALU.mult),
                                        reads=[pO.b, x2h.b], writes=[yaB[hf * 2 + cch][tt]])
                self.barrier()


def _t5_bucket(rel):
    half = 16
    ret = np.where(rel > 0, half, 0)
    n = np.abs(rel)
    nf = np.maximum(n, 1).astype(np.float32)
    large = 8 + (np.log(nf / np.float32(8)) / np.float32(math.log(1024 / 8)) * np.float32(half - 8)).astype(np.int32)
    large = np.minimum(large, half - 1)
    return ret + np.where(n < 8, n, large)


def _col(v):
    v = np.asarray(v, np.float32).reshape(-1, 128)
    return np.ascontiguousarray(v.T)


_CONST = {}


def _constants():
    if _CONST:
        return _CONST
    L = T
    f32 = np.float32
    t = np.linspace(0.0, 1.0, L, dtype=f32)[:, None]
    tr = np.arange(L, dtype=f32)[:, None]
    wpos = (f32(2.0 * math.pi) * tr / f32(L)).astype(f32)
    fb = np.linspace(1e-4, 15, 16, dtype=f32)[None, :]
    ang = (fb * wpos).astype(f32)
    z = np.concatenate([t, np.cos(ang.astype(np.float64)).astype(f32), -np.sin(ang.astype(np.float64)).astype(f32)], axis=-1)
    _CONST["zT"] = np.ascontiguousarray(z.T.astype(f32))
    deltas = np.abs(np.linspace(math.log(1e-2) / 0.3, math.log(1e-2) / 1.5, 512, dtype=f32))
    _CONST["delta_bc"] = np.ascontiguousarray(np.broadcast_to(deltas[None, :], (128, 512))).astype(f32)
    tt = np.linspace(0.0, 1.0, L, dtype=f32)
    _CONST["tcol"] = np.ascontiguousarray((-tt).reshape(16, 128).T).astype(f32)
    N = 2 * L
    tt_ = np.arange(L, dtype=np.int64)
    ff = np.arange(NF * 128, dtype=np.int64)
    idx = (tt_[:, None] * ff[None, :]) % N
    ang = idx.astype(np.float64) * (2.0 * math.pi / N)
    valid = (ff <= L)[None, :]
    C = np.where(valid, np.cos(ang), 0.0)
    S = np.where(valid, -np.sin(ang), 0.0)
    csf = np.stack([C, S], 0)
    csf = csf.reshape(2, 16, 128, NF, 128)
    _CONST["csf"] = np.ascontiguousarray(csf.transpose(3, 2, 0, 1, 4)).astype(ml_dtypes.bfloat16)
    scale = np.where((ff == 0) | (ff == L), 1.0 / N, 2.0 / N) * (ff <= L)
    Ci = (C * scale[None, :]).T
    Si = (S * scale[None, :]).T
    csi = np.stack([Ci, Si], 1)
    csi = csi.reshape(NF, 128, 2, 2, L // 2)
    _CONST["csi"] = np.ascontiguousarray(csi.transpose(0, 3, 1, 2, 4)).astype(ml_dtypes.bfloat16)
    _CONST["ident"] = np.eye(128, dtype=np.float32).astype(ml_dtypes.bfloat16)
    bidx = []
    for g, (w_, d) in enumerate(GROUPS):
        Ls = L // d
        off = 64 if Ls > 128 else 0
        kk = np.arange(128)[:, None]
        qq = np.arange(256)[None, :]
        delta = kk - qq + off
        ok = np.abs(delta) <= 64
        if not off:
            ok = ok & (qq < 128)
        bidx.append((_t5_bucket((delta * d).astype(np.int32)), ok))
    _CONST["bidx"] = bidx
    return _CONST


def prep_shared(inp):
    c = _constants()
    f32 = np.float32
    sh = {}
    for n in ("ffn1_wg", "ffn1_wu", "ffn1_wd", "ffn2_wg", "ffn2_wu", "ffn2_wd", "w_in", "w_gate", "w_proj_hy",
              "w_proj_attn", "w_proj_rg", "w_out", "hy_w1", "hy_w2", "hy_w3", "hy_wout"):
        sh[n] = np.ascontiguousarray(np.asarray(inp[n], f32))
    pcol = np.zeros((4, 128, NPC), f32)
    for l in range(4):
        pcol[l, :, PC["n1"]:PC["n1"] + 8] = _col(inp["ffn1_norm"][l])
        pcol[l, :, PC["nm"]:PC["nm"] + 8] = _col(inp["mix_norm"][l])
        pcol[l, :, PC["n2"]:PC["n2"] + 8] = _col(inp["ffn2_norm"][l])
        pcol[l, :, PC["bg"]:PC["bg"] + 24] = _col(inp["b_gate"][l])
        for k in range(3):
            pcol[l, :, PC["hcw"] + k * 12:PC["hcw"] + (k + 1) * 12] = _col(inp["hy_conv_w"][l, k])
        pcol[l, :, PC["hcb"]:PC["hcb"] + 12] = _col(inp["hy_conv_b"][l])
        for k in range(4):
            pcol[l, :, PC["rcw"] + k * 4:PC["rcw"] + (k + 1) * 4] = _col(inp["rg_conv_w"][l, k])
        pcol[l, :, PC["rcb"]:PC["rcb"] + 4] = _col(inp["rg_conv_b"][l])
        for dd in range(2):
            pcol[l, :, PC["rba"] + dd * 4:PC["rba"] + (dd + 1) * 4] = _col(inp["rg_ba"][l, dd])
            pcol[l, :, PC["rbx"] + dd * 4:PC["rbx"] + (dd + 1) * 4] = _col(inp["rg_bx"][l, dd])
            pcol[l, :, PC["rlam"] + dd * 4:PC["rlam"] + (dd + 1) * 4] = _col(inp["rg_lambda"][l, dd])
        pcol[l, 0:64, PC["hb1"]] = inp["hy_b1"][l]
        pcol[l, 0:64, PC["hb2"]] = inp["hy_b2"][l]
        pcol[l, 0:64, PC["hb3"]] = inp["hy_b3"][l]
        pcol[l, 0:64, PC["hfr"]] = inp["hy_freq"][l]
    sh["pcol"] = pcol
    sh["fncol"] = _col(inp["final_norm"])
    sh["zT"] = c["zT"]
    sh["delta_bc"] = c["delta_bc"]
    sh["tcol"] = c["tcol"]
    sh["skip_bc"] = np.ascontiguousarray(np.broadcast_to(np.asarray(inp["hy_skip"], f32)[:, :, None, :], (4, 2, 128, 512)))
    bd = np.zeros((4, 2, 2, 4, 128, 128), f32)
    for wi, nm in enumerate(("rg_wa", "rg_wx")):
        w = np.asarray(inp[nm], f32)
        for cc in range(4):
            bd[:, wi, :, cc, 0:64, 0:64] = w[:, :, 2 * cc]
            bd[:, wi, :, cc, 64:128, 64:128] = w[:, :, 2 * cc + 1]
    sh["rg_bd"] = bd
    rb = np.asarray(inp["rel_bias"], f32)
    ab = np.empty((3, 128, 8, 256), f32)
    for g in range(3):
        bi, ok = c["bidx"][g]
        tab = rb[:, g * 8:(g + 1) * 8][bi]
        tab = np.where(ok[:, :, None], tab, f32(-30000.0))
        ab[g] = tab.transpose(0, 2, 1)
    sh["attn_bias"] = ab
    sh["csf"] = c["csf"]
    sh["csi"] = c["csi"]
    sh["ident"] = c["ident"]
    return sh


_NC_CACHE = {}


def get_nc(n_seq, layers, flags=None, nl=4):
    key = (n_seq, tuple(layers), tuple(sorted((flags or {}).items())), nl)
    if key not in _NC_CACHE:
        b = Builder(n_seq, layers, flags, nl)
        nc = b.nc
        g = b.es
        b.eps = b.sb(g, [128, 1], F32)
        b.one = b.sb(g, [128, 1], F32)
        b.op(b.dve, lambda: nc.vector.memset(b.eps.t[:], 1e-6), writes=[b.eps.b])
        b.op(b.dve, lambda: nc.vector.memset(b.one.t[:], 1.0), writes=[b.one.b])
        b.build()
        _NC_CACHE[key] = (b, nc)
    return _NC_CACHE[key][1]


LAYER_KEYS = ("ffn1_wg", "ffn1_wu", "ffn1_wd", "ffn2_wg", "ffn2_wu", "ffn2_wd", "w_in", "w_gate", "w_proj_hy",
              "w_proj_attn", "w_proj_rg", "w_out", "hy_w1", "hy_w2", "hy_w3", "hy_wout", "pcol", "skip_bc", "rg_bd")
FUSED = True


def _pad_layer(a, l):
    return np.ascontiguousarray(a[l:l + 1])


def kernel(**inputs):
    x = np.asarray(inputs["x"], np.float32)
    B = x.shape[0]
    n_cores = 8
    per = B // n_cores
    sh = prep_shared(inputs)
    out = np.empty_like(x)
    if FUSED:
        nc = get_nc(per, [0, 1, 2, 3])
        in_maps = []
        for c in range(n_cores):
            m = dict(sh)
            m["xT"] = np.ascontiguousarray(x[c * per:(c + 1) * per].transpose(0, 2, 1))
            in_maps.append(m)
        res = run_bass_kernel_spmd(nc, in_maps, core_ids=list(range(n_cores)))
        for c in range(n_cores):
            out[c * per:(c + 1) * per] = np.asarray(res.results[c]["yT"], np.float32).transpose(0, 2, 1)
        return out
    nc_layer = get_nc(1, [0], {"final": False}, nl=1)
    nc_final = get_nc(1, [], {"final": True}, nl=1)
    shl = []
    for l in range(4):
        m = dict(sh)
        for k in LAYER_KEYS:
            m[k] = _pad_layer(sh[k], l)
        shl.append(m)
    for s in range(per):
        hT = [np.ascontiguousarray(x[c * per + s:c * per + s + 1].transpose(0, 2, 1)) for c in range(n_cores)]
        for l in range(4):
            in_maps = []
            for c in range(n_cores):
                m = dict(shl[l])
                m["xT"] = hT[c]
                in_maps.append(m)
            res = run_bass_kernel_spmd(nc_layer, in_maps, core_ids=list(range(n_cores)))
            hT = [np.ascontiguousarray(np.asarray(res.results[c]["yT"], np.float32)) for c in range(n_cores)]
        in_maps = []
        for c in range(n_cores):
            m = dict(shl[0])
            m["xT"] = hT[c]
            in_maps.append(m)
        res = run_bass_kernel_spmd(nc_final, in_maps, core_ids=list(range(n_cores)))
        for c in range(n_cores):
            out[c * per + s] = np.asarray(res.results[c]["yT"], np.float32)[0].T
    return out
```

```python
import math
from contextlib import ExitStack

import numpy as np
import ml_dtypes
import concourse.bass as bass
import concourse.mybir as mybir
from concourse.bass_utils import run_bass_kernel_spmd

F32 = mybir.dt.float32
BF16 = mybir.dt.bfloat16
AF = mybir.ActivationFunctionType
ALU = mybir.AluOpType

D = 1024
T = 2048
DFF = 2816
NF = 17
GROUPS = ((128, 1), (512, 4), (2048, 16))
PC = dict(n1=0, nm=8, n2=16, bg=24, hcw=48, hcb=84, rcw=96, rcb=112, rba=116, rbx=124, rlam=132,
          hb1=140, hb2=141, hb3=142, hfr=143)
NPC = 144


class Sem:
    def __init__(self, h):
        self.h = h
        self.total = 0


class Eng:
    def __init__(self, name, eng, sem, is_pe=False):
        self.name = name
        self.eng = eng
        self.sem = sem
        self.is_pe = is_pe
        self.seen = {}
        self.dirty = False


class Buf:
    __slots__ = ("w", "r")

    def __init__(self):
        self.w = None
        self.r = {}


class Tile:
    def __init__(self, t):
        self.t = t
        self.b = Buf()


class Builder:
    def __init__(self, n_seq, layers, flags=None, nl=4):
        self.nl = nl
        self.n_seq = n_seq
        self.layers = layers
        self.flags = flags or {}
        self.nc = bass.Bass("TRN2", target_bir_lowering=False)
        self.es = ExitStack()
        self.nsem = 0
        nc = self.nc
        self.pe = Eng("pe", nc.tensor, self.mksem(), True)
        self.act = Eng("act", nc.scalar, self.mksem())
        self.dve = Eng("dve", nc.vector, self.mksem())
        self.pool = Eng("pool", nc.gpsimd, self.mksem())
        self.sp = Eng("sp", nc.sync, self.mksem())
        self.dsems = [self.mksem() for _ in range(24)]
        self.dnext = 0

    def mksem(self):
        self.nsem += 1
        return Sem(self.es.enter_context(self.nc.semaphore(f"s{self.nsem}")))

    def _wait(self, E, deps):
        for sem, val in deps:
            if val <= 0 or E.seen.get(sem, 0) >= val:
                continue
            assert val <= sem.total, f"wait on un-issued milestone ({E.name})"
            E.eng.wait_ge(sem.h, val)
            E.seen[sem] = val

    def op(self, E, fn, reads=(), writes=(), inc=True):
        deps = []
        for b in reads:
            if b.w is not None and not (E.is_pe and b.w[0] is E.sem):
                deps.append(b.w)
        for b in writes:
            if b.w is not None and not (E.is_pe and b.w[0] is E.sem):
                deps.append(b.w)
            for sem, val in b.r.items():
                if E.is_pe and sem is E.sem:
                    continue
                deps.append((sem, val))
        self._wait(E, deps)
        inst = fn()
        if inc:
            E.sem.total += 1
            inst.then_inc(E.sem.h, 1)
            st = (E.sem, E.sem.total)
            E.dirty = False
        else:
            st = (E.sem, E.sem.total + 1)
            E.dirty = True
        for b in reads:
            b.r[st[0]] = max(b.r.get(st[0], 0), st[1])
        for b in writes:
            b.w = st
            b.r = {}
        return inst

    def dma(self, Q, out, in_, reads=(), writes=()):
        sem = self.dsems[self.dnext]
        self.dnext = (self.dnext + 1) % len(self.dsems)
        deps = [(sem, sem.total)]
        for b in reads:
            if b.w is not None:
                deps.append(b.w)
        for b in writes:
            if b.w is not None:
                deps.append(b.w)
            deps.extend(b.r.items())
        self._wait(Q, deps)
        Q.eng.dma_start(out=out, in_=in_).then_inc(sem.h, 16)
        sem.total += 16
        st = (sem, sem.total)
        for b in reads:
            b.r[sem] = st[1]
        for b in writes:
            b.w = st
            b.r = {}

    def barrier(self):
        comp = [self.pe, self.act, self.dve, self.pool]
        assert not self.pe.dirty
        for E in [self.pe, self.act, self.dve, self.pool, self.sp]:
            self._wait(E, [(X.sem, X.sem.total) for X in comp if not (X is E and E.is_pe)])
        for E in comp:
            if E.sem.total > 24000:
                E.sem = self.mksem()

    def sb(self, st, shape, dt):
        self.ntile = getattr(self, "ntile", 0) + 1
        return Tile(st.enter_context(self.nc.sbuf_tensor(f"t{self.ntile}", list(shape), dt)))

    def build(self):
        nc = self.nc
        g = self.es
        dr = lambda n, s, dt=F32, kind="ExternalInput": nc.dram_tensor(n, list(s), dt, kind=kind).ap()
        NS = self.n_seq
        NL = self.nl
        self.xT = dr("xT", [NS, D, T])
        self.yT = dr("yT", [NS, D, T], kind="ExternalOutput")
        self.w = {}
        for f in ("ffn1", "ffn2"):
            self.w[f + "_wg"] = dr(f + "_wg", [NL, D, DFF])
            self.w[f + "_wu"] = dr(f + "_wu", [NL, D, DFF])
            self.w[f + "_wd"] = dr(f + "_wd", [NL, DFF, D])
        self.w["w_in"] = dr("w_in", [NL, D, 7168])
        self.w["w_gate"] = dr("w_gate", [NL, D, 3072])
        for n in ("w_proj_hy", "w_proj_attn", "w_proj_rg"):
            self.w[n] = dr(n, [NL, 512, D])
        self.w["w_out"] = dr("w_out", [NL, D, D])
        self.d_pcol = dr("pcol", [NL, 128, NPC])
        self.d_fn = dr("fncol", [128, 8])
        self.d_w1 = dr("hy_w1", [NL, 33, 64])
        self.d_w2 = dr("hy_w2", [NL, 64, 64])
        self.d_w3 = dr("hy_w3", [NL, 64, 64])
        self.d_wout = dr("hy_wout", [NL, 64, 2048])
        self.d_zT = dr("zT", [33, T])
        self.d_delta = dr("delta_bc", [128, 512])
        self.d_tcol = dr("tcol", [128, 16])
        self.d_skip = dr("skip_bc", [NL, 2, 128, 512])
        self.d_bd = dr("rg_bd", [NL, 2, 2, 4, 128, 128])
        self.d_ab = dr("attn_bias", [3, 128, 8, 256])
        self.d_csf = dr("csf", [NF, 128, 2, 16, 128], BF16)
        self.d_csi = dr("csi", [NF, 2, 128, 2, 1024], BF16)
        self.d_ident = dr("ident", [128, 128], BF16)

        self.h = self.sb(g, [128, 8, T], F32)
        self.hB = [[Buf() for _ in range(4)] for _ in range(8)]
        self.xn = self.sb(g, [128, 8, T], BF16)
        self.xnB = [[Buf() for _ in range(4)] for _ in range(8)]
        self.ones32 = self.sb(g, [128, 128], F32)
        self.onesb = self.sb(g, [128, 3, 64], BF16)
        self.ident = self.sb(g, [128, 128], BF16)
        self.pcol = self.sb(g, [128, NPC], F32)
        self.fncol = self.sb(g, [128, 8], F32)
        self.sq = [self.sb(g, [128, 512], F32) for _ in range(2)]
        self.rstd2 = [self.sb(g, [128, 512], F32) for _ in range(2)]
        self.acc = Tile(g.enter_context(nc.psum_tensor("acc", [128, 2048], F32)))
        self.ps = []
        for i in range(4):
            tl = Tile(None)
            tl.ap = self.acc.t[:, i * 512:(i + 1) * 512]
            self.ps.append(tl)
        for i in range(4, 8):
            tl = Tile(g.enter_context(nc.psum_tensor(f"ps{i}", [128, 512], F32)))
            tl.ap = tl.t[:]
            self.ps.append(tl)

        self.op(self.dve, lambda: nc.vector.memset(self.ones32.t[:], 1.0), writes=[self.ones32.b])
        self.op(self.dve, lambda: nc.vector.memset(self.onesb.t[:], 1.0), writes=[self.onesb.b])
        self.op(self.dve, lambda: nc.vector.memset(self.onesb.t[0:64, 0, :], 0.0), writes=[self.onesb.b])
        self.op(self.dve, lambda: nc.vector.memset(self.onesb.t[64:128, 2, :], 0.0), writes=[self.onesb.b])
        self.dma(self.sp, self.ident.t[:], self.d_ident[:, :], writes=[self.ident.b])
        self.dma(self.sp, self.fncol.t[:], self.d_fn[:, :], writes=[self.fncol.b])

        for s in range(NS):
            for kc in range(8):
                self.dma(self.sp, self.h.t[:, kc, :], self.xT[s, kc * 128:(kc + 1) * 128, :], writes=self.hB[kc])
            for l in self.layers:
                self.dma(self.sp, self.pcol.t[:], self.d_pcol[l], writes=[self.pcol.b])
                if self.flags.get("ffn1", True):
                    self.ffn(l, "ffn1", PC["n1"])
                if self.flags.get("mix", True):
                    self.mixer(l)
                if self.flags.get("ffn2", True):
                    self.ffn(l, "ffn2", PC["n2"])
                self.barrier()
            self.final(s)
            self.barrier()
        self._wait(self.sp, [(sm, sm.total) for sm in self.dsems])
        return nc

    def rmsnorm(self, gcol, out_fn=None, save=None, load=None):
        nc = self.nc
        if load is not None:
            for tt in range(4):
                sl = slice(tt * 512, (tt + 1) * 512)
                for kc in range(8):
                    self.op(self.dve, lambda: nc.vector.scalar_tensor_tensor(
                        out=self.xn.t[:, kc, sl], in0=self.h.t[:, kc, sl], scalar=gcol[:, kc:kc + 1],
                        in1=load.t[:, sl], op0=ALU.mult, op1=ALU.mult),
                        reads=[self.hB[kc][tt], load.b, self.pcol.b], writes=[self.xnB[kc][tt]])
            return
        for tt in range(4):
            sl = slice(tt * 512, (tt + 1) * 512)
            pst = self.ps[6 + tt % 2]
            rs = self.rstd2[tt % 2]
            for kc in range(8):
                sq = self.sq[kc % 2]
                self.op(self.act, lambda: nc.scalar.activation(out=sq.t[:], in_=self.h.t[:, kc, sl], func=AF.Square),
                        reads=[self.hB[kc][tt]], writes=[sq.b])
                self.op(self.pe, lambda: nc.tensor.matmul(pst.ap, lhsT=self.ones32.t[:], rhs=sq.t[:],
                                                          start=(kc == 0), stop=(kc == 7)),
                        reads=[sq.b, self.ones32.b], writes=[pst.b])
            self.op(self.act, lambda: nc.scalar.activation(out=rs.t[:], in_=pst.ap, func=AF.Ln,
                                                           bias=self.eps.t[:, 0:1], scale=1.0 / D),
                    reads=[pst.b, self.eps.b], writes=[rs.b])
            self.op(self.act, lambda: nc.scalar.activation(out=rs.t[:], in_=rs.t[:], func=AF.Exp, scale=-0.5),
                    reads=[rs.b], writes=[rs.b])
            if save is not None:
                self.op(self.act, lambda: nc.scalar.activation(out=save.t[:, sl], in_=rs.t[:], func=AF.Copy),
                        reads=[rs.b], writes=[save.b])
            for kc in range(8):
                if out_fn is None:
                    self.op(self.dve, lambda: nc.vector.scalar_tensor_tensor(
                        out=self.xn.t[:, kc, sl], in0=self.h.t[:, kc, sl], scalar=gcol[:, kc:kc + 1],
                        in1=rs.t[:], op0=ALU.mult, op1=ALU.mult),
                        reads=[self.hB[kc][tt], rs.b, self.pcol.b], writes=[self.xnB[kc][tt]])
                else:
                    out_fn(kc, tt, sl, rs)

    def final(self, s):
        nc = self.nc
        if not self.flags.get("final", True):
            for kc in range(8):
                self.dma(self.sp, self.yT[s, kc * 128:(kc + 1) * 128, :], self.h.t[:, kc, :], reads=self.hB[kc])
            return
        with ExitStack() as ph:
            st = [self.sb(ph, [128, 512], F32) for _ in range(3)]
            cnt = [0]

            def out_fn(kc, tt, sl, rs):
                o = st[cnt[0] % 3]
                cnt[0] += 1
                self.op(self.dve, lambda: nc.vector.scalar_tensor_tensor(
                    out=o.t[:], in0=self.h.t[:, kc, sl], scalar=self.fncol.t[:, kc:kc + 1],
                    in1=rs.t[:], op0=ALU.mult, op1=ALU.mult),
                    reads=[self.hB[kc][tt], rs.b, self.fncol.b], writes=[o.b])
                self.dma(self.sp, self.yT[s, kc * 128:(kc + 1) * 128, sl], o.t[:], reads=[o.b])

            self.rmsnorm(None, out_fn)
            self.barrier()
            self._wait(self.sp, [(sm, sm.total) for sm in self.dsems])
            for E in (self.pe, self.act, self.dve, self.pool):
                pass

    def ffn(self, l, name, ncol):
        nc = self.nc
        wg_d, wu_d, wd_d = self.w[name + "_wg"], self.w[name + "_wu"], self.w[name + "_wd"]
        self.rmsnorm(self.pcol.t[:, ncol:ncol + 8])
        pieces = [(0, 4), (4, 4), (8, 4), (12, 4), (16, 4), (20, 2)]
        with ExitStack() as ph:
            slots = []
            for _ in range(2):
                slots.append(dict(wg=self.sb(ph, [128, 8, 512], BF16), wu=self.sb(ph, [128, 8, 512], BF16),
                                  wd=self.sb(ph, [128, 4, D], BF16)))
            hT = [self.sb(ph, [128, 4, 512], BF16) for _ in range(2)]
            hTB = [[Buf() for _ in range(4)] for _ in range(2)]
            sg = [self.sb(ph, [128, 512], F32) for _ in range(2)]

            def load(pi):
                j0, nj = pieces[pi]
                sl = slots[pi % 2]
                ncl = nj * 128
                self.dma(self.pool, sl["wg"].t[:, :, 0:ncl],
                         wg_d[l].rearrange("(kc p) c -> p kc c", p=128)[:, :, j0 * 128:j0 * 128 + ncl],
                         writes=[sl["wg"].b])
                self.dma(self.pool, sl["wu"].t[:, :, 0:ncl],
                         wu_d[l].rearrange("(kc p) c -> p kc c", p=128)[:, :, j0 * 128:j0 * 128 + ncl],
                         writes=[sl["wu"].b])
                self.dma(self.pool, sl["wd"].t[:, 0:nj, :],
                         wd_d[l, j0 * 128:(j0 + nj) * 128, :].rearrange("(j p) m -> p j m", p=128),
                         writes=[sl["wd"].b])

            load(0)
            load(1)
            cg = [0]
            units = [(pi, tt) for pi in range(len(pieces)) for tt in range(4)]

            def gu(k):
                pi, tt = units[k]
                j0, nj = pieces[pi]
                sl = slots[pi % 2]
                tsl = slice(tt * 512, (tt + 1) * 512)
                ht = hT[k % 2]
                hb = hTB[k % 2]
                for j in range(nj):
                    pg = self.ps[(2 * cg[0]) % 4]
                    pu = self.ps[(2 * cg[0] + 1) % 4]
                    sgt = sg[cg[0] % 2]
                    cg[0] += 1
                    for kc in range(8):
                        self.op(self.pe, lambda: nc.tensor.matmul(
                            pg.ap, lhsT=sl["wg"].t[:, kc, j * 128:(j + 1) * 128], rhs=self.xn.t[:, kc, tsl],
                            start=(kc == 0), stop=(kc == 7)),
                            reads=[sl["wg"].b, self.xnB[kc][tt]], writes=[pg.b], inc=(kc == 7))
                    for kc in range(8):
                        self.op(self.pe, lambda: nc.tensor.matmul(
                            pu.ap, lhsT=sl["wu"].t[:, kc, j * 128:(j + 1) * 128], rhs=self.xn.t[:, kc, tsl],
                            start=(kc == 0), stop=(kc == 7)),
                            reads=[sl["wu"].b, self.xnB[kc][tt]], writes=[pu.b], inc=(kc == 7))
                    self.op(self.act, lambda: nc.scalar.activation(out=sgt.t[:], in_=pg.ap, func=AF.Silu),
                            reads=[pg.b], writes=[sgt.b])
                    self.op(self.dve, lambda: nc.vector.tensor_tensor(out=ht.t[:, j, :], in0=pu.ap, in1=sgt.t[:],
                                                                      op=ALU.mult),
                            reads=[pu.b, sgt.b], writes=[hb[j]])

            def down(k):
                pi, tt = units[k]
                j0, nj = pieces[pi]
                sl = slots[pi % 2]
                tsl = slice(tt * 512, (tt + 1) * 512)
                ht = hT[k % 2]
                hb = hTB[k % 2]
                for m in range(8):
                    po = self.ps[4 + m % 2]
                    for j in range(nj):
                        self.op(self.pe, lambda: nc.tensor.matmul(
                            po.ap, lhsT=sl["wd"].t[:, j, m * 128:(m + 1) * 128], rhs=ht.t[:, j, :],
                            start=(j == 0), stop=(j == nj - 1)),
                            reads=[sl["wd"].b, hb[j]], writes=[po.b], inc=(j == nj - 1))
                    self.op(self.dve, lambda: nc.vector.scalar_tensor_tensor(
                        out=self.h.t[:, m, tsl], in0=po.ap, scalar=0.5, in1=self.h.t[:, m, tsl],
                        op0=ALU.mult, op1=ALU.add),
                        reads=[po.b, self.hB[m][tt]], writes=[self.hB[m][tt]])
                if tt == 3 and pi + 2 < len(pieces):
                    load(pi + 2)

            for k in range(len(units) + 1):
                if k < len(units):
                    gu(k)
                if k >= 1:
                    down(k - 1)
            self.barrier()

    def mixer(self, l):
        fl = self.flags
        if fl.get("hy", True):
            self.branch_hy(l)
        else:
            self.rmsnorm(self.pcol.t[:, PC["nm"]:PC["nm"] + 8])
        if fl.get("rg", True):
            self.branch_rg(l)
        if fl.get("attn", True):
            self.branch_attn(l)

    def epilogue(self, l, b, ph, y, yB, wproj_name):
        nc = self.nc
        with ExitStack() as ep:
            wp = self.sb(ep, [128, 4, D], BF16)
            wgt = self.sb(ep, [128, 8, D], BF16)
            wo = self.sb(ep, [128, 8, D], BF16)
            tmp = [self.sb(ep, [128, 8, 512], BF16) for _ in range(2)]
            tmpB = [[Buf() for _ in range(8)] for _ in range(2)]
            gt = [self.sb(ep, [128, 512], F32) for _ in range(2)]
            wpB = [Buf() for _ in range(8)]
            wgB = [Buf() for _ in range(8)]
            woB = [Buf() for _ in range(8)]
            wpd = self.w[wproj_name][l].rearrange("(kc p) m -> p kc m", p=128)
            wgd = self.w["w_gate"][l].rearrange("(kc p) m -> p kc m", p=128)
            wod = self.w["w_out"][l].rearrange("(kc p) m -> p kc m", p=128)
            for m in range(8):
                msl = slice(m * 128, (m + 1) * 128)
                self.dma(self.pool, wp.t[:, :, msl], wpd[:, :, msl], writes=[wpB[m]])
                self.dma(self.pool, wgt.t[:, :, msl], wgd[:, :, b * D + m * 128: b * D + (m + 1) * 128], writes=[wgB[m]])
            for m in range(8):
                msl = slice(m * 128, (m + 1) * 128)
                self.dma(self.pool, wo.t[:, :, msl], wod[:, :, msl], writes=[woB[m]])
            def pg_(tt):
                tsl = slice(tt * 512, (tt + 1) * 512)
                tm = tmp[tt % 2]
                tb = tmpB[tt % 2]
                for m in range(8):
                    pP = self.ps[m % 2]
                    pG = self.ps[2 + m % 2]
                    g_ = gt[m % 2]
                    for kc in range(4):
                        self.op(self.pe, lambda: nc.tensor.matmul(
                            pP.ap, lhsT=wp.t[:, kc, m * 128:(m + 1) * 128], rhs=y.t[:, kc, tsl],
                            start=(kc == 0), stop=(kc == 3)), reads=[wpB[m], yB[kc][tt]], writes=[pP.b], inc=(kc == 3))
                    for kc in range(8):
                        self.op(self.pe, lambda: nc.tensor.matmul(
                            pG.ap, lhsT=wgt.t[:, kc, m * 128:(m + 1) * 128], rhs=self.xn.t[:, kc, tsl],
                            start=(kc == 0), stop=(kc == 7)), reads=[wgB[m], self.xnB[kc][tt]], writes=[pG.b],
                            inc=(kc == 7))
                    bcol = PC["bg"] + b * 8 + m
                    self.op(self.act, lambda: nc.scalar.activation(out=g_.t[:], in_=pG.ap, func=AF.Sigmoid,
                                                                   bias=self.pcol.t[:, bcol:bcol + 1], scale=1.0),
                            reads=[pG.b, self.pcol.b], writes=[g_.b])
                    self.op(self.dve, lambda: nc.vector.tensor_tensor(out=tm.t[:, m, :], in0=pP.ap, in1=g_.t[:],
                                                                      op=ALU.mult),
                            reads=[pP.b, g_.b], writes=[tb[m]])

            def out_(tt):
                tsl = slice(tt * 512, (tt + 1) * 512)
                tm = tmp[tt % 2]
                tb = tmpB[tt % 2]
                for mo in range(8):
                    pO = self.ps[4 + mo % 2]
                    for m in range(8):
                        self.op(self.pe, lambda: nc.tensor.matmul(
                            pO.ap, lhsT=wo.t[:, m, mo * 128:(mo + 1) * 128], rhs=tm.t[:, m, :],
                            start=(m == 0), stop=(m == 7)), reads=[woB[mo], tb[m]], writes=[pO.b], inc=(m == 7))
                    self.op(self.dve, lambda: nc.vector.tensor_tensor(out=self.h.t[:, mo, tsl], in0=pO.ap,
                                                                      in1=self.h.t[:, mo, tsl], op=ALU.add),
                            reads=[pO.b, self.hB[mo][tt]], writes=[self.hB[mo][tt]])

            for tt in range(5):
                if tt < 4:
                    pg_(tt)
                if tt >= 1:
                    out_(tt - 1)
            self.barrier()

    def branch_rg(self, l):
        nc = self.nc
        with ExitStack() as ph:
            yc = self.sb(ph, [128, 4, T], BF16)
            ycB = [[Buf() for _ in range(4)] for _ in range(4)]
            with ExitStack() as p1:
                wx = [self.sb(p1, [128, 8, 128], BF16) for _ in range(2)]
                wgt = [self.sb(p1, [128, 8, 128], BF16) for _ in range(2)]
                bd = [self.sb(p1, [128, 4, 128], BF16) for _ in range(2)]
                bdB = [[Buf() for _ in range(4)] for _ in range(2)]
                xpad = self.sb(p1, [128, T + 4], F32)
                xc = self.sb(p1, [128, T], F32)
                xcb = self.sb(p1, [128, T], BF16)
                ra = self.sb(p1, [128, T], F32)
                gu = self.sb(p1, [128, T], F32)
                t1 = self.sb(p1, [128, T], F32)
                hs = [self.sb(p1, [128, T], F32) for _ in range(2)]
                gg = self.sb(p1, [128, T], BF16)
                gx = [self.sb(p1, [128, 512], F32) for _ in range(2)]
                gi_ = [self.sb(p1, [128, 512], F32) for _ in range(2)]
                ccol = self.sb(p1, [128, 8], F32)
                lam = self.pcol.t[:, PC["rlam"]:PC["rlam"] + 8]
                self.op(self.act, lambda: nc.scalar.activation(out=ccol.t[:], in_=lam, func=AF.Exp, scale=-1.0),
                        reads=[self.pcol.b], writes=[ccol.b])
                self.op(self.act, lambda: nc.scalar.activation(out=ccol.t[:], in_=ccol.t[:], func=AF.Ln,
                                                               bias=self.one.t[:, 0:1], scale=1.0),
                        reads=[ccol.b, self.one.b], writes=[ccol.b])
                self.op(self.dve, lambda: nc.vector.tensor_scalar(out=ccol.t[:], in0=ccol.t[:], scalar1=-8.0,
                                                                  scalar2=None, op0=ALU.mult),
                        reads=[ccol.b], writes=[ccol.b])
                self.op(self.dve, lambda: nc.vector.memset(xpad.t[:, 0:2], 0.0), writes=[xpad.b])
                self.op(self.dve, lambda: nc.vector.memset(xpad.t[:, T + 2:T + 4], 0.0), writes=[xpad.b])
                base = 1536 + 4608
                win = self.w["w_in"][l].rearrange("(kc p) c -> p kc c", p=128)

                def load(cc):
                    self.dma(self.pool, wx[cc % 2].t[:], win[:, :, base + cc * 128: base + (cc + 1) * 128],
                             writes=[wx[cc % 2].b])
                    self.dma(self.pool, wgt[cc % 2].t[:], win[:, :, base + 512 + cc * 128: base + 512 + (cc + 1) * 128],
                             writes=[wgt[cc % 2].b])
                    for i, (w_, d_) in enumerate(((0, 0), (0, 1), (1, 0), (1, 1))):
                        self.dma(self.pool, bd[cc % 2].t[:, i, :], self.d_bd[l, w_, d_, cc], writes=[bdB[cc % 2][i]])

                load(0)
                for cc in range(4):
                    if cc + 1 < 4:
                        load(cc + 1)
                    wx_, wg_, bd_ = wx[cc % 2], wgt[cc % 2], bd[cc % 2]
                    for tt in range(4):
                        tsl = slice(tt * 512, (tt + 1) * 512)
                        pX = self.ps[tt % 2]
                        pG = self.ps[2 + tt % 2]
                        for kc in range(8):
                            self.op(self.pe, lambda: nc.tensor.matmul(pX.ap, lhsT=wx_.t[:, kc, :], rhs=self.xn.t[:, kc, tsl],
                                                                      start=(kc == 0), stop=(kc == 7)),
                                    reads=[wx_.b, self.xnB[kc][tt]], writes=[pX.b], inc=(kc == 7))
                        for kc in range(8):
                            self.op(self.pe, lambda: nc.tensor.matmul(pG.ap, lhsT=wg_.t[:, kc, :], rhs=self.xn.t[:, kc, tsl],
                                                                      start=(kc == 0), stop=(kc == 7)),
                                    reads=[wg_.b, self.xnB[kc][tt]], writes=[pG.b], inc=(kc == 7))
                        self.op(self.act, lambda: nc.scalar.activation(out=xpad.t[:, 2 + tt * 512: 2 + (tt + 1) * 512],
                                                                       in_=pX.ap, func=AF.Copy),
                                reads=[pX.b], writes=[xpad.b])
                        gx_ = gx[tt % 2]
                        gq = gi_[tt % 2]
                        self.op(self.act, lambda: nc.scalar.activation(out=gx_.t[:], in_=pG.ap, func=AF.Copy),
                                reads=[pG.b], writes=[gx_.b])
                        self.op(self.pool, lambda: nc.gpsimd.tensor_tensor(out=gq.t[:], in0=gx_.t[:], in1=gx_.t[:], op=ALU.mult),
                                reads=[gx_.b], writes=[gq.b])
                        self.op(self.pool, lambda: nc.gpsimd.tensor_scalar(out=gq.t[:], in0=gq.t[:], scalar1=0.044715,
                                                                           scalar2=1.0, op0=ALU.mult, op1=ALU.add),
                                reads=[gq.b], writes=[gq.b])
                        self.op(self.pool, lambda: nc.gpsimd.tensor_tensor(out=gq.t[:], in0=gq.t[:], in1=gx_.t[:], op=ALU.mult),
                                reads=[gq.b, gx_.b], writes=[gq.b])
                        self.op(self.act, lambda: nc.scalar.activation(out=gq.t[:], in_=gq.t[:], func=AF.Sigmoid,
                                                                       scale=1.5957691216057308),
                                reads=[gq.b], writes=[gq.b])
                        self.op(self.pool, lambda: nc.gpsimd.tensor_tensor(out=gg.t[:, tsl], in0=gq.t[:], in1=gx_.t[:], op=ALU.mult),
                                reads=[gq.b, gx_.b], writes=[gg.b])
                    cw = lambda k: self.pcol.t[:, PC["rcw"] + k * 4 + cc: PC["rcw"] + k * 4 + cc + 1]
                    cb = self.pcol.t[:, PC["rcb"] + cc: PC["rcb"] + cc + 1]
                    self.op(self.dve, lambda: nc.vector.tensor_scalar(out=xc.t[:], in0=xpad.t[:, 0:T], scalar1=cw(0),
                                                                      scalar2=cb, op0=ALU.mult, op1=ALU.add),
                            reads=[xpad.b, self.pcol.b], writes=[xc.b])
                    for k in range(1, 4):
                        self.op(self.dve, lambda: nc.vector.scalar_tensor_tensor(
                            out=xc.t[:], in0=xpad.t[:, k:k + T], scalar=cw(k), in1=xc.t[:], op0=ALU.mult, op1=ALU.add),
                            reads=[xpad.b, xc.b, self.pcol.b], writes=[xc.b])
                    self.op(self.act, lambda: nc.scalar.activation(out=xcb.t[:], in_=xc.t[:], func=AF.Copy),
                            reads=[xc.b], writes=[xcb.b])
                    for dr_ in range(2):
                        order = [0, 1, 2, 3] if dr_ == 0 else [3, 2, 1, 0]
                        raB = [Buf() for _ in range(4)]
                        guB = [Buf() for _ in range(4)]
                        t1B = [Buf() for _ in range(4)]
                        for lst, whole in ((raB, ra.b), (guB, gu.b), (t1B, t1.b)):
                            for bb in lst:
                                bb.w = whole.w
                                bb.r = dict(whole.r)
                        for tt in order:
                            tsl = slice(tt * 512, (tt + 1) * 512)
                            pR = self.ps[4 + tt % 2]
                            pI = self.ps[6 + tt % 2]
                            self.op(self.pe, lambda: nc.tensor.matmul(pR.ap, lhsT=bd_.t[:, dr_, :], rhs=xcb.t[:, tsl],
                                                                      start=True, stop=True),
                                    reads=[bdB[cc % 2][dr_], xcb.b], writes=[pR.b])
                            self.op(self.pe, lambda: nc.tensor.matmul(pI.ap, lhsT=bd_.t[:, 2 + dr_, :], rhs=xcb.t[:, tsl],
                                                                      start=True, stop=True),
                                    reads=[bdB[cc % 2][2 + dr_], xcb.b], writes=[pI.b])
                            ba = self.pcol.t[:, PC["rba"] + dr_ * 4 + cc: PC["rba"] + dr_ * 4 + cc + 1]
                            bx = self.pcol.t[:, PC["rbx"] + dr_ * 4 + cc: PC["rbx"] + dr_ * 4 + cc + 1]
                            self.op(self.act, lambda: nc.scalar.activation(out=ra.t[:, tsl], in_=pR.ap, func=AF.Sigmoid,
                                                                           bias=ba, scale=1.0),
                                    reads=[pR.b, self.pcol.b], writes=[raB[tt]])
                            self.op(self.act, lambda: nc.scalar.activation(out=gu.t[:, tsl], in_=pI.ap, func=AF.Sigmoid,
                                                                           bias=bx, scale=1.0),
                                    reads=[pI.b, self.pcol.b], writes=[guB[tt]])
                        cc_ = ccol.t[:, dr_ * 4 + cc: dr_ * 4 + cc + 1]
                        hsd = hs[dr_]
                        hsB = [Buf() for _ in range(4)]
                        for bb in hsB:
                            bb.w = hsd.b.w
                            bb.r = dict(hsd.b.r)
                        prev = None
                        for tt in order:
                            tsl = slice(tt * 512, (tt + 1) * 512)
                            self.op(self.act, lambda: nc.scalar.activation(out=ra.t[:, tsl], in_=ra.t[:, tsl], func=AF.Exp, scale=cc_),
                                    reads=[raB[tt], ccol.b], writes=[raB[tt]])
                            self.op(self.dve, lambda: nc.vector.tensor_tensor(out=t1.t[:, tsl], in0=ra.t[:, tsl], in1=ra.t[:, tsl], op=ALU.mult),
                                    reads=[raB[tt]], writes=[t1B[tt]])
                            self.op(self.act, lambda: nc.scalar.activation(out=t1.t[:, tsl], in_=t1.t[:, tsl], func=AF.Ln,
                                                                           bias=self.one.t[:, 0:1], scale=-1.0),
                                    reads=[t1B[tt], self.one.b], writes=[t1B[tt]])
                            self.op(self.act, lambda: nc.scalar.activation(out=t1.t[:, tsl], in_=t1.t[:, tsl], func=AF.Exp, scale=0.5),
                                    reads=[t1B[tt]], writes=[t1B[tt]])
                            self.op(self.dve, lambda: nc.vector.tensor_tensor(out=gu.t[:, tsl], in0=gu.t[:, tsl], in1=xc.t[:, tsl], op=ALU.mult),
                                    reads=[guB[tt], xc.b], writes=[guB[tt]])
                            self.op(self.dve, lambda: nc.vector.tensor_tensor(out=gu.t[:, tsl], in0=gu.t[:, tsl], in1=t1.t[:, tsl], op=ALU.mult),
                                    reads=[guB[tt], t1B[tt]], writes=[guB[tt]])
                            if dr_ == 0:
                                init = 0.0 if prev is None else hsd.t[:, tt * 512 - 1: tt * 512]
                                self.op(self.dve, lambda: nc.vector.tensor_tensor_scan(
                                    out=hsd.t[:, tsl], data0=ra.t[:, tsl], data1=gu.t[:, tsl], initial=init,
                                    op0=ALU.mult, op1=ALU.add),
                                    reads=[raB[tt], guB[tt]] + ([hsB[prev]] if prev is not None else []), writes=[hsB[tt]])
                            else:
                                init = 0.0 if prev is None else hsd.t[:, (tt + 1) * 512: (tt + 1) * 512 + 1]
                                rsl = slice((tt + 1) * 512 - 1, tt * 512 - 1 if tt > 0 else None, -1)
                                self.op(self.dve, lambda: nc.vector.tensor_tensor_scan(
                                    out=hsd.t[:, rsl], data0=ra.t[:, rsl], data1=gu.t[:, rsl], initial=init,
                                    op0=ALU.mult, op1=ALU.add),
                                    reads=[raB[tt], guB[tt]] + ([hsB[prev]] if prev is not None else []), writes=[hsB[tt]])
                            prev = tt
                        def fold(whole, lst):
                            r = {}
                            w = None
                            for bb in lst:
                                if bb.w is not None:
                                    assert w is None or w[0] is bb.w[0]
                                    if w is None or bb.w[1] > w[1]:
                                        w = bb.w
                                for sm, v in bb.r.items():
                                    r[sm] = max(r.get(sm, 0), v)
                            whole.w = w
                            whole.r = r
                        for whole, lst in ((ra.b, raB), (gu.b, guB), (t1.b, t1B), (hsd.b, hsB)):
                            fold(whole, lst)
                    self.op(self.dve, lambda: nc.vector.tensor_tensor(out=hs[0].t[:], in0=hs[0].t[:], in1=hs[1].t[:], op=ALU.add),
                            reads=[hs[0].b, hs[1].b], writes=[hs[0].b])
                    self.op(self.dve, lambda: nc.vector.tensor_tensor(out=yc.t[:, cc, :], in0=hs[0].t[:], in1=gg.t[:], op=ALU.mult),
                            reads=[hs[0].b, gg.b], writes=ycB[cc])
                self.barrier()
            self.epilogue(l, 2, ph, yc, ycB, "w_proj_rg")

    def branch_attn(self, l):
        nc = self.nc
        win = self.w["w_in"][l].rearrange("(kc p) c -> p kc c", p=128)
        with ExitStack() as ph:
            yb = self.sb(ph, [128, 4, T], BF16)
            ybB = [[Buf() for _ in range(4)] for _ in range(4)]
            with ExitStack() as p1:
                geo = []
                for (win_, d) in GROUPS:
                    Ls = T // d
                    nb = Ls // 128
                    off = 64 if nb > 1 else 0
                    nch = nb + (1 if off else 0)
                    geo.append((d, Ls, nb, off, nch))
                QT = [self.sb(p1, [128, T], BF16) for _ in range(3)]
                KT = [self.sb(p1, [128, geo[g][0] * (geo[g][1] + 2 * geo[g][3])], BF16) for g in range(3)]
                VT = [self.sb(p1, [128, geo[g][0] * geo[g][4], 128], BF16) for g in range(3)]
                Eb = [self.sb(p1, [128, 8, 256 if geo[g_][3] else 128], BF16) for g_ in range(3)]
                with ExitStack() as pe_:
                    est = self.sb(pe_, [128, 8, 256], F32)
                    for g in range(3):
                        self.dma(self.sp, est.t[:], self.d_ab[g], writes=[est.b])
                        ew = 256 if geo[g][3] else 128
                        self.op(self.act, lambda: nc.scalar.activation(out=Eb[g].t[:], in_=est.t[:, :, 0:ew], func=AF.Exp),
                                reads=[est.b], writes=[Eb[g].b])
                    self.barrier()
                wq = [self.sb(p1, [128, 8, 128], BF16) for _ in range(2)]
                wk = [self.sb(p1, [128, 8, 128], BF16) for _ in range(2)]
                wv = [self.sb(p1, [128, 8, 128], BF16) for _ in range(2)]
                tot = [self.sb(p1, [128, T], F32) for _ in range(2)]
                rec = self.sb(p1, [64, T], F32)
                pt = [self.sb(p1, [128, 256], BF16) for _ in range(3)]
                psSB = [Buf() for _ in range(4)]
                for g in range(3):
                    self.op(self.dve, lambda: nc.vector.memset(KT[g].t[:], 0.0), writes=[KT[g].b])
                    self.op(self.dve, lambda: nc.vector.memset(VT[g].t[:], 0.0), writes=[VT[g].b])
                li = 0
                loads = [(hp, g) for hp in range(4) for g in range(3)]

                def load(i):
                    hp, g = loads[i]
                    for qi, wt in enumerate((wq, wk, wv)):
                        c0 = 1536 + ((qi * 3 + g) * 8) * 64 + hp * 128
                        self.dma(self.pool, wt[i % 2].t[:], win[:, :, c0:c0 + 128], writes=[wt[i % 2].b])

                load(0)
                pcnt = [0]
                for hp in range(4):
                    for g in range(3):
                        i = hp * 3 + g
                        if i + 1 < len(loads):
                            load(i + 1)
                        d, Ls, nb, off, nch = geo[g]
                        wq_, wk_, wv_ = wq[i % 2], wk[i % 2], wv[i % 2]
                        QTv = QT[g].t[:].rearrange("p (r j) -> p r j", r=d)
                        KTv = KT[g].t[:].rearrange("p (r j) -> p r j", r=d)
                        for tt in range(4):
                            tsl = slice(tt * 512, (tt + 1) * 512)
                            for which, (w_, dstv, poff) in enumerate(((wq_, QTv, 0), (wk_, KTv, off))):
                                pp = self.ps[6 + (tt * 2 + which) % 2]
                                for kc in range(8):
                                    self.op(self.pe, lambda: nc.tensor.matmul(pp.ap, lhsT=w_.t[:, kc, :],
                                                                              rhs=self.xn.t[:, kc, tsl],
                                                                              start=(kc == 0), stop=(kc == 7)),
                                            reads=[w_.b, self.xnB[kc][tt]], writes=[pp.b], inc=(kc == 7))
                                nj = 512 // d
                                j0 = tt * nj
                                dst = dstv[:, :, poff + j0: poff + j0 + nj]
                                src = pp.ap.rearrange("p (j r) -> p r j", r=d)
                                dB = QT[g].b if which == 0 else KT[g].b
                                if which == 0:
                                    self.op(self.act, lambda: nc.scalar.activation(out=dst, in_=src, func=AF.Copy),
                                            reads=[pp.b], writes=[dB])
                                else:
                                    self.op(self.dve, lambda: nc.vector.tensor_copy(out=dst, in_=src),
                                            reads=[pp.b], writes=[dB])
                        vcnt = 0
                        for r in range(d):
                            for c in range(nch):
                                jlo = max(0, 128 * c - off)
                                jhi = min(Ls, 128 * c - off + 128)
                                kk0 = jlo - (128 * c - off)
                                nr = jhi - jlo
                                ci = r * nch + c
                                pv = self.ps[6 + vcnt % 2]
                                vcnt += 1
                                t0 = r + d * jlo
                                for kc in range(8):
                                    lhsT = self.xn.t[:, kc, t0: t0 + d * (nr - 1) + 1: d]
                                    self.op(self.pe, lambda: nc.tensor.matmul(
                                        pv.ap[kk0:kk0 + nr, 0:128], lhsT=lhsT, rhs=wv_.t[:, kc, :],
                                        start=(kc == 0), stop=(kc == 7)),
                                        reads=[wv_.b] + [self.xnB[kc][q] for q in range(4)], writes=[pv.b], inc=(kc == 7))
                                if vcnt % 2:
                                    self.op(self.act, lambda: nc.scalar.activation(
                                        out=VT[g].t[kk0:kk0 + nr, ci, :], in_=pv.ap[kk0:kk0 + nr, 0:128], func=AF.Copy),
                                        reads=[pv.b], writes=[VT[g].b])
                                else:
                                    self.op(self.dve, lambda: nc.vector.tensor_copy(
                                        out=VT[g].t[kk0:kk0 + nr, ci, :], in_=pv.ap[kk0:kk0 + nr, 0:128]),
                                        reads=[pv.b], writes=[VT[g].b])
                        tiles = [(hh, r, c) for hh in range(2) for r in range(d) for c in range(nch)]
                        LA = 2
                        started = [set(), set()]
                        info = {}

                        def front(k):
                            hh, r, c = tiles[k]
                            hd = 2 * hp + hh
                            prt = slice(hh * 64, (hh + 1) * 64)
                            blocks = [bi for bi in ((c - 1, c) if off else (c,)) if 0 <= bi < nb]
                            qa = 128 * blocks[0]
                            nq = 128 * len(blocks)
                            ecol0 = qa - 128 * (c - 1) if off else 0
                            kq = pcnt[0] % 4
                            pS_ap = self.ps[4 + kq].ap[:, 0:nq]
                            pSb = self.ps[4 + kq].b
                            p_ = pt[pcnt[0] % len(pt)]
                            pcnt[0] += 1
                            info[k] = (p_, blocks)
                            self.op(self.pe, lambda: nc.tensor.matmul(
                                pS_ap, lhsT=KTv[prt, r, 128 * c: 128 * c + 128],
                                rhs=QTv[prt, r, qa: qa + nq], start=True, stop=True),
                                reads=[KT[g].b, QT[g].b], writes=[pSb])
                            self.op(self.act, lambda: nc.scalar.activation(out=p_.t[:, 0:nq], in_=pS_ap,
                                                                           func=AF.Exp, scale=0.125),
                                    reads=[pSb], writes=[p_.b])
                            self.op(self.dve, lambda: nc.vector.tensor_tensor(
                                out=p_.t[:, 0:nq], in0=p_.t[:, 0:nq], in1=Eb[g].t[:, hd, ecol0: ecol0 + nq],
                                op=ALU.mult), reads=[p_.b, Eb[g].b], writes=[p_.b])

                        def back(k):
                            hh, r, c = tiles[k]
                            prt = slice(hh * 64, (hh + 1) * 64)
                            ci = r * nch + c
                            p_, blocks = info.pop(k)
                            if off:
                                ov = 0 if c == 0 else (2 if c == nb else 1)
                            else:
                                ov = 1
                            for bi, blk in enumerate(blocks):
                                col0 = r * Ls + 128 * blk
                                bank = col0 // 512
                                rhs = p_.t[:, bi * 128:(bi + 1) * 128]
                                s1 = ("n", bank) not in started[hh]
                                started[hh].add(("n", bank))
                                self.op(self.pe, lambda: nc.tensor.matmul(
                                    self.acc.t[0:64, col0: col0 + 128], lhsT=VT[g].t[:, ci, prt], rhs=rhs,
                                    start=s1, stop=True, skip_group_check=True),
                                    reads=[VT[g].b, p_.b], writes=[self.acc.b], inc=False)
                                s2 = ("d", bank) not in started[hh]
                                started[hh].add(("d", bank))
                                self.op(self.pe, lambda: nc.tensor.matmul(
                                    self.acc.t[64:128, col0: col0 + 128], lhsT=self.onesb.t[:, ov, :], rhs=rhs,
                                    start=s2, stop=True, skip_group_check=True),
                                    reads=[self.onesb.b, p_.b], writes=[self.acc.b], inc=True)
                            if r == d - 1 and c == nch - 1:
                                tt_ = tot[hh]
                                if g == 0:
                                    self.op(self.act, lambda: nc.scalar.activation(out=tt_.t[:], in_=self.acc.t[:], func=AF.Copy),
                                            reads=[self.acc.b], writes=[tt_.b])
                                else:
                                    tv = tt_.t[:].rearrange("p (j r) -> p r j", r=d)
                                    av = self.acc.t[:].rearrange("p (r j) -> p r j", r=d)
                                    self.op(self.dve, lambda: nc.vector.tensor_tensor(out=tv, in0=av, in1=tv, op=ALU.add),
                                            reads=[self.acc.b, tt_.b], writes=[tt_.b])

                        for k in range(len(tiles) + LA):
                            if k < len(tiles):
                                front(k)
                            if k >= LA:
                                back(k - LA)
                    for hh in range(2):
                        tt_ = tot[hh]
                        self.op(self.act, lambda: nc.scalar.activation(out=rec.t[0:64, :], in_=tt_.t[64:128, :], func=AF.Ln),
                                reads=[tt_.b], writes=[rec.b])
                        self.op(self.act, lambda: nc.scalar.activation(out=rec.t[0:64, :], in_=rec.t[0:64, :], func=AF.Exp, scale=-1.0),
                                reads=[rec.b], writes=[rec.b])
                        self.op(self.dve, lambda: nc.vector.tensor_tensor(
                            out=yb.t[hh * 64:(hh + 1) * 64, hp, :], in0=tt_.t[0:64, :], in1=rec.t[0:64, :], op=ALU.mult),
                            reads=[tt_.b, rec.b], writes=ybB[hp])
                self.barrier()
            self.epilogue(l, 1, ph, yb, ybB, "w_proj_attn")

    def branch_hy(self, l):
        nc = self.nc
        win = self.w["w_in"][l].rearrange("(kc p) c -> p kc c", p=128)
        with ExitStack() as ph:
            ya = self.sb(ph, [128, 4, T], BF16)
            yaB = [[Buf() for _ in range(4)] for _ in range(4)]
            with ExitStack() as p1:
                rall = self.sb(p1, [128, T], F32)
                self.rmsnorm(self.pcol.t[:, PC["nm"]:PC["nm"] + 8], save=rall)
                hdn3 = self.sb(p1, [64, T], BF16)
                with ExitStack() as p0:
                    zt = self.sb(p0, [33, T], F32)
                    w1 = self.sb(p0, [33, 64], F32)
                    w2 = self.sb(p0, [64, 64], F32)
                    w3 = self.sb(p0, [64, 64], F32)
                    ha = self.sb(p0, [64, T], F32)
                    hb_ = self.sb(p0, [64, T], F32)
                    s_ = [self.sb(p0, [64, 512], F32) for _ in range(2)]
                    s2 = [self.sb(p0, [64, 512], F32) for _ in range(2)]
                    cols = self.sb(p0, [64, 4], F32)
                    self.dma(self.sp, zt.t[:], self.d_zT[:, :], writes=[zt.b])
                    self.dma(self.sp, w1.t[:], self.d_w1[l], writes=[w1.b])
                    self.dma(self.sp, w2.t[:], self.d_w2[l], writes=[w2.b])
                    self.dma(self.sp, w3.t[:], self.d_w3[l], writes=[w3.b])
                    fr = self.pcol.t[0:64, PC["hfr"]:PC["hfr"] + 1]
                    self.op(self.dve, lambda: nc.vector.tensor_scalar(out=cols.t[:, 0:1], in0=fr, scalar1=1.0 / 3.0,
                                                                      scalar2=None, op0=ALU.mult),
                            reads=[self.pcol.b], writes=[cols.b])
                    for k in range(3):
                        bk = self.pcol.t[0:64, PC["hb1"] + k: PC["hb1"] + k + 1]
                        self.op(self.dve, lambda: nc.vector.tensor_tensor(out=cols.t[:, k + 1:k + 2], in0=bk,
                                                                          in1=cols.t[:, 0:1], op=ALU.mult),
                                reads=[self.pcol.b, cols.b], writes=[cols.b])
                    srcs = [(zt, w1, 33), (ha, w2, 64), (hb_, w3, 64)]
                    dsts = [ha, hb_, hdn3]
                    for k in range(3):
                        src, wk_, kk = srcs[k]
                        dst = dsts[k]
                        for tt in range(4):
                            tsl = slice(tt * 512, (tt + 1) * 512)
                            pm = self.ps[tt % 2]
                            sa, sb2 = s_[tt % 2], s2[tt % 2]
                            self.op(self.pe, lambda: nc.tensor.matmul(pm.ap[0:64, :], lhsT=wk_.t[0:kk, :], rhs=src.t[0:kk, tsl],
                                                                      start=True, stop=True),
                                    reads=[wk_.b, src.b], writes=[pm.b])
                            self.op(self.act, lambda: nc.scalar.activation(out=sa.t[:], in_=pm.ap[0:64, :], func=AF.Sin,
                                                                           bias=cols.t[:, k + 1:k + 2], scale=cols.t[:, 0:1]),
                                    reads=[pm.b, cols.b], writes=[sa.b])
                            self.op(self.dve, lambda: nc.vector.tensor_tensor(out=sb2.t[:], in0=sa.t[:], in1=sa.t[:], op=ALU.mult),
                                    reads=[sa.b], writes=[sb2.b])
                            self.op(self.dve, lambda: nc.vector.tensor_scalar(out=sb2.t[:], in0=sb2.t[:], scalar1=-4.0,
                                                                              scalar2=3.0, op0=ALU.mult, op1=ALU.add),
                                    reads=[sb2.b], writes=[sb2.b])
                            self.op(self.dve, lambda: nc.vector.tensor_tensor(out=dst.t[:, tsl], in0=sb2.t[:], in1=sa.t[:], op=ALU.mult),
                                    reads=[sb2.b, sa.b], writes=[dst.b])
                    self.barrier()
                for hf in range(2):
                    if hf == 1:
                        self.rmsnorm(self.pcol.t[:, PC["nm"]:PC["nm"] + 8], load=rall)
                    self.hy_half(l, hf, p1, hdn3, ya, yaB, win)
                self.rmsnorm(self.pcol.t[:, PC["nm"]:PC["nm"] + 8], load=rall)
                self.barrier()
            self.epilogue(l, 0, ph, ya, yaB, "w_proj_hy")

    def hy_half(self, l, hf, p1, hdn3, ya, yaB, win):
        nc = self.nc
        with ExitStack() as hh:
            ZK = self.sb(hh, [128, 16, 3, 256], BF16)
            zB = [Buf() for _ in range(16)]
            kB = Buf()
            x1h = self.sb(hh, [128, 2, T], BF16)
            x2h = self.sb(hh, [128, 2, T], BF16)
            with ExitStack() as pin:
                wsl = [self.sb(pin, [128, 8, 128], BF16) for _ in range(2)]
                upads = [self.sb(pin, [128, T + 2], F32) for _ in range(2)]
                uc = self.sb(pin, [128, T], F32)
                vfm = self.sb(pin, [128, T], BF16)
                for upad in upads:
                    self.op(self.dve, lambda: nc.vector.memset(upad.t[:, 0:1], 0.0), writes=[upad.b])
                    self.op(self.dve, lambda: nc.vector.memset(upad.t[:, T + 1:T + 2], 0.0), writes=[upad.b])
                items = [(part, cch) for part in range(3) for cch in range(2)]

                def load(i):
                    part, cch = items[i]
                    c0 = part * 512 + hf * 256 + cch * 128
                    self.dma(self.pool, wsl[i % 2].t[:], win[:, :, c0:c0 + 128], writes=[wsl[i % 2].b])

                load(0)
                def inproj(i):
                    part, cch = items[i]
                    if i + 1 < len(items):
                        load(i + 1)
                    w_ = wsl[i % 2]
                    upad = upads[i % 2]
                    for tt in range(4):
                        tsl = slice(tt * 512, (tt + 1) * 512)
                        pp = self.ps[tt % 4]
                        for kc in range(8):
                            self.op(self.pe, lambda: nc.tensor.matmul(pp.ap, lhsT=w_.t[:, kc, :], rhs=self.xn.t[:, kc, tsl],
                                                                      start=(kc == 0), stop=(kc == 7)),
                                    reads=[w_.b, self.xnB[kc][tt]], writes=[pp.b], inc=(kc == 7))
                        self.op(self.act, lambda: nc.scalar.activation(out=upad.t[:, 1 + tt * 512: 1 + (tt + 1) * 512],
                                                                       in_=pp.ap, func=AF.Copy),
                                reads=[pp.b], writes=[upad.b])

                def convpart(i):
                    part, cch = items[i]
                    upad = upads[i % 2]
                    gch = part * 4 + hf * 2 + cch
                    cw = lambda k: self.pcol.t[:, PC["hcw"] + k * 12 + gch: PC["hcw"] + k * 12 + gch + 1]
                    cb = self.pcol.t[:, PC["hcb"] + gch: PC["hcb"] + gch + 1]
                    self.op(self.dve, lambda: nc.vector.tensor_scalar(out=uc.t[:], in0=upad.t[:, 0:T], scalar1=cw(0),
                                                                      scalar2=cb, op0=ALU.mult, op1=ALU.add),
                            reads=[upad.b, self.pcol.b], writes=[uc.b])
                    self.op(self.dve, lambda: nc.vector.scalar_tensor_tensor(
                        out=uc.t[:], in0=upad.t[:, 1:1 + T], scalar=cw(1), in1=uc.t[:], op0=ALU.mult, op1=ALU.add),
                        reads=[upad.b, uc.b, self.pcol.b], writes=[uc.b])
                    if part == 0:
                        dst, dB = vfm.t[:], [vfm.b]
                    elif part == 1:
                        dst, dB = x1h.t[:, cch, :], [x1h.b]
                    else:
                        dst, dB = x2h.t[:, cch, :], [x2h.b]
                    self.op(self.dve, lambda: nc.vector.scalar_tensor_tensor(
                        out=dst, in0=upad.t[:, 2:2 + T], scalar=cw(2), in1=uc.t[:], op0=ALU.mult, op1=ALU.add),
                        reads=[upad.b, uc.b, self.pcol.b], writes=dB)
                    if part == 0:
                        for q in range(4):
                            pT = self.ps[4 + q % 2]
                            pTb = pT.ap.bitcast(BF16)
                            for k4 in range(4):
                                tc_ = q * 4 + k4
                                self.op(self.pe, lambda: nc.tensor.transpose(
                                    pTb[:, k4 * 128:(k4 + 1) * 128], vfm.t[:, tc_ * 128:(tc_ + 1) * 128], self.ident.t[:]),
                                    reads=[vfm.b, self.ident.b], writes=[pT.b], inc=(k4 == 3))
                            self.op(self.act, lambda: nc.scalar.activation(
                                out=ZK.t[:, q * 4:(q + 1) * 4, 0, cch * 128:(cch + 1) * 128],
                                in_=pTb[:, 0:512].rearrange("p (k c) -> p k c", k=4), func=AF.Copy),
                                reads=[pT.b], writes=zB[q * 4:(q + 1) * 4])

                for i in range(len(items) + 1):
                    if i < len(items):
                        inproj(i)
                    if i >= 1:
                        convpart(i - 1)
                self.barrier()
            with ExitStack() as pm:
                xflat = self.xn.t[:].rearrange("p a b -> p (a b)")
                Y = Tile(None)
                Y.t = xflat[:, 0:NF * 512].rearrange("p (f r c) -> p f r c", f=NF, r=2)
                YB = [Buf() for _ in range(NF)]
                wo_ = self.sb(pm, [64, 2, 256], BF16)
                woB2 = [Buf(), Buf()]
                dl = self.sb(pm, [128, 256], F32)
                tcl = self.sb(pm, [128, 16], F32)
                skb = self.sb(pm, [128, 256], F32)
                rn = self.sb(pm, [128, 256], F32)
                dec = [self.sb(pm, [128, 256], F32) for _ in range(3)]
                kf = [self.sb(pm, [128, 256], F32) for _ in range(3)]
                kb = [self.sb(pm, [128, 256], F32) for _ in range(3)]
                ab = [self.sb(pm, [128, 256], F32) for _ in range(3)]
                abacc = self.sb(pm, [128, 256], F32)
                kr, ki, ta, tb = dec, kf, kb, ab
                fsl = [self.sb(pm, [128, 2, 16, 128], BF16) for _ in range(2)]
                isl = []
                for i_ in range(3):
                    tl_ = Tile(None)
                    tl_.t = xflat[:, NF * 512 + i_ * 2048: NF * 512 + (i_ + 1) * 2048].rearrange("p (r c) -> p r c", r=2)
                    isl.append(tl_)
                zfm = []
                for i_ in range(2):
                    tl_ = Tile(None)
                    z0 = NF * 512 + 3 * 2048 + i_ * 512
                    tl_.t = xflat[:, z0:z0 + 512]
                    zfm.append(tl_)
                self.dma(self.sp, dl.t[:], self.d_delta[:, hf * 256:(hf + 1) * 256], writes=[dl.b])
                self.dma(self.sp, tcl.t[:], self.d_tcol[:, :], writes=[tcl.b])
                for o in range(2):
                    for dr_ in range(2):
                        c0 = dr_ * 1024 + o * 512 + hf * 256
                        self.dma(self.pool, wo_.t[:, dr_, :], self.d_wout[l, :, c0:c0 + 256], writes=[woB2[dr_]])
                    self.dma(self.sp, skb.t[:], self.d_skip[l, o, :, hf * 256:(hf + 1) * 256], writes=[skb.b])
                    pN = self.ps[7]

                    def stA(tc_):
                        i3 = tc_ % 3
                        pK = self.ps[i3]
                        self.op(self.pe, lambda: nc.tensor.matmul(pK.ap, lhsT=hdn3.t[0:64, tc_ * 128:(tc_ + 1) * 128],
                                                                  rhs=wo_.t[0:64, :, :], start=True, stop=True),
                                reads=[hdn3.b, woB2[0], woB2[1]], writes=[pK.b])
                        self.op(self.act, lambda: nc.scalar.activation(out=dec[i3].t[:], in_=dl.t[:], func=AF.Exp,
                                                                       scale=tcl.t[:, tc_:tc_ + 1]),
                                reads=[dl.b, tcl.b], writes=[dec[i3].b])

                    def stB(tc_):
                        i3 = tc_ % 3
                        pK = self.ps[i3]
                        self.op(self.dve, lambda: nc.vector.scalar_tensor_tensor(
                            out=kf[i3].t[:], in0=dec[i3].t[:], scalar=0.05, in1=pK.ap[:, 0:256], op0=ALU.add, op1=ALU.mult),
                            reads=[dec[i3].b, pK.b], writes=[kf[i3].b])
                        self.op(self.dve, lambda: nc.vector.scalar_tensor_tensor(
                            out=kb[i3].t[:], in0=dec[i3].t[:], scalar=0.05, in1=pK.ap[:, 256:512], op0=ALU.add, op1=ALU.mult),
                            reads=[dec[i3].b, pK.b], writes=[kb[i3].b])
                        if tc_ == 0:
                            self.op(self.dve, lambda: nc.vector.memset(kb[i3].t[0:1, :], 0.0), reads=[kb[i3].b], writes=[kb[i3].b])

                    def stC(tc_):
                        i3 = tc_ % 3
                        self.op(self.pool, lambda: nc.gpsimd.tensor_tensor(out=ZK.t[:, tc_, 1, :], in0=kf[i3].t[:], in1=kb[i3].t[:], op=ALU.add),
                                reads=[kf[i3].b, kb[i3].b], writes=[kB])
                        self.op(self.pool, lambda: nc.gpsimd.tensor_tensor(out=ZK.t[:, tc_, 2, :], in0=kf[i3].t[:], in1=kb[i3].t[:], op=ALU.subtract),
                                reads=[kf[i3].b, kb[i3].b], writes=[kB])
                        self.op(self.act, lambda: nc.scalar.activation(out=ab[i3].t[:], in_=kf[i3].t[:], func=AF.Abs),
                                reads=[kf[i3].b], writes=[ab[i3].b])
                        self.op(self.act, lambda: nc.scalar.activation(out=dec[i3].t[:], in_=kb[i3].t[:], func=AF.Abs),
                                reads=[kb[i3].b], writes=[dec[i3].b])
                        if tc_ == 0:
                            self.op(self.dve, lambda: nc.vector.tensor_tensor(out=abacc.t[:], in0=ab[i3].t[:], in1=dec[i3].t[:], op=ALU.add),
                                    reads=[ab[i3].b, dec[i3].b], writes=[abacc.b])
                        else:
                            self.op(self.dve, lambda: nc.vector.tensor_tensor(out=ab[i3].t[:], in0=ab[i3].t[:], in1=dec[i3].t[:], op=ALU.add),
                                    reads=[ab[i3].b, dec[i3].b], writes=[ab[i3].b])
                            self.op(self.pool, lambda: nc.gpsimd.tensor_tensor(out=abacc.t[:], in0=abacc.t[:], in1=ab[i3].t[:], op=ALU.add),
                                    reads=[abacc.b, ab[i3].b], writes=[abacc.b])

                    for st_ in range(16 + 2):
                        if st_ < 16:
                            stA(st_)
                        if 0 <= st_ - 1 < 16:
                            stB(st_ - 1)
                        if 0 <= st_ - 2 < 16:
                            stC(st_ - 2)
                    self.op(self.pe, lambda: nc.tensor.matmul(pN.ap[:, 0:256], lhsT=self.ones32.t[:], rhs=abacc.t[:],
                                                              start=True, stop=True),
                            reads=[abacc.b, self.ones32.b], writes=[pN.b])
                    self.op(self.dve, lambda: nc.vector.tensor_scalar(out=rn.t[:], in0=pN.ap[:, 0:256], scalar1=1e-6,
                                                                      scalar2=None, op0=ALU.add),
                            reads=[pN.b], writes=[rn.b])
                    self.op(self.dve, lambda: nc.vector.reciprocal(out=rn.t[:], in_=rn.t[:]), reads=[rn.b], writes=[rn.b])
                    self.dma(self.sp, fsl[0].t[:], self.d_csf[0], writes=[fsl[0].b])
                    for fc in range(NF):
                        if fc + 1 < NF:
                            self.dma(self.sp, fsl[(fc + 1) % 2].t[:], self.d_csf[fc + 1], writes=[fsl[(fc + 1) % 2].b])
                        fs_ = fsl[fc % 2]
                        pA = self.ps[(2 * fc) % 4]
                        pB = self.ps[(2 * fc + 1) % 4]
                        for tc_ in range(16):
                            self.op(self.pe, lambda: nc.tensor.matmul(pA.ap, lhsT=fs_.t[:, 0, tc_, :], rhs=ZK.t[:, tc_, 0:2, :],
                                                                      start=(tc_ == 0), stop=(tc_ == 15)),
                                    reads=[fs_.b, zB[tc_], kB], writes=[pA.b], inc=(tc_ == 15))
                        for tc_ in range(16):
                            self.op(self.pe, lambda: nc.tensor.matmul(pB.ap, lhsT=fs_.t[:, 1, tc_, :], rhs=ZK.t[:, tc_, 0:3:2, :],
                                                                      start=(tc_ == 0), stop=(tc_ == 15)),
                                    reads=[fs_.b, zB[tc_], kB], writes=[pB.b], inc=(tc_ == 15))
                        i2 = fc % 2
                        V = nc.vector
                        self.op(self.dve, lambda: V.tensor_tensor(out=kr[i2].t[:], in0=pA.ap[:, 256:512], in1=rn.t[:], op=ALU.mult),
                                reads=[pA.b, rn.b], writes=[kr[i2].b])
                        self.op(self.dve, lambda: V.tensor_tensor(out=kr[i2].t[:], in0=kr[i2].t[:], in1=skb.t[:], op=ALU.add),
                                reads=[kr[i2].b, skb.b], writes=[kr[i2].b])
                        self.op(self.dve, lambda: V.tensor_tensor(out=ki[i2].t[:], in0=pB.ap[:, 256:512], in1=rn.t[:], op=ALU.mult),
                                reads=[pB.b, rn.b], writes=[ki[i2].b])
                        self.op(self.dve, lambda: V.tensor_tensor(out=ta[i2].t[:], in0=pA.ap[:, 0:256], in1=kr[i2].t[:], op=ALU.mult),
                                reads=[pA.b, kr[i2].b], writes=[ta[i2].b])
                        self.op(self.dve, lambda: V.tensor_tensor(out=tb[i2].t[:], in0=pB.ap[:, 0:256], in1=ki[i2].t[:], op=ALU.mult),
                                reads=[pB.b, ki[i2].b], writes=[tb[i2].b])
                        self.op(self.dve, lambda: V.tensor_tensor(out=Y.t[:, fc, 0, :], in0=ta[i2].t[:], in1=tb[i2].t[:], op=ALU.subtract),
                                reads=[ta[i2].b, tb[i2].b], writes=[YB[fc]])
                        self.op(self.dve, lambda: V.tensor_tensor(out=ta[i2].t[:], in0=pA.ap[:, 0:256], in1=ki[i2].t[:], op=ALU.mult),
                                reads=[pA.b, ki[i2].b, ta[i2].b], writes=[ta[i2].b])
                        self.op(self.dve, lambda: V.tensor_tensor(out=tb[i2].t[:], in0=pB.ap[:, 0:256], in1=kr[i2].t[:], op=ALU.mult),
                                reads=[pB.b, kr[i2].b, tb[i2].b], writes=[tb[i2].b])
                        self.op(self.dve, lambda: V.tensor_tensor(out=Y.t[:, fc, 1, :], in0=ta[i2].t[:], in1=tb[i2].t[:], op=ALU.add),
                                reads=[ta[i2].b, tb[i2].b], writes=[YB[fc]])
                    icnt = 0
                    for th in range(2):
                        seq = [(fc) for fc in range(NF)]
                        self.dma(self.sp, isl[icnt % 3].t[:], self.d_csi[0, th],
                                 writes=[isl[icnt % 3].b])
                        for fc in range(NF):
                            if fc + 1 < NF:
                                self.dma(self.sp, isl[(icnt + 1) % 3].t[:], self.d_csi[fc + 1, th],
                                         writes=[isl[(icnt + 1) % 3].b])
                            is_ = isl[icnt % 3]
                            icnt += 1
                            for cch in range(2):
                                for t2 in range(2):
                                    pO = self.ps[cch * 2 + t2]
                                    for ri in range(2):
                                        last = (fc == NF - 1 and ri == 1)
                                        self.op(self.pe, lambda: nc.tensor.matmul(
                                            pO.ap, lhsT=Y.t[:, fc, ri, cch * 128:(cch + 1) * 128],
                                            rhs=is_.t[:, ri, t2 * 512:(t2 + 1) * 512],
                                            start=(fc == 0 and ri == 0), stop=last),
                                            reads=[YB[fc], is_.b], writes=[pO.b], inc=(last or (cch == 1 and t2 == 1 and ri == 1)))
                        for cch in range(2):
                            for t2 in range(2):
                                pO = self.ps[cch * 2 + t2]
                                tt = th * 2 + t2
                                tsl = slice(tt * 512, (tt + 1) * 512)
                                if o == 0:
                                    zf = zfm[(cch * 2 + t2) % 2]
                                    self.op(self.dve, lambda: nc.vector.tensor_tensor(out=zf.t[:], in0=pO.ap, in1=x1h.t[:, cch, tsl], op=ALU.mult),
                                            reads=[pO.b, x1h.b], writes=[zf.b])
                                    pT = self.ps[4 + (cch * 2 + t2) % 2]
                                    pTb = pT.ap.bitcast(BF16)
                                    for k4 in range(4):
                                        self.op(self.pe, lambda: nc.tensor.transpose(
                                            pTb[:, k4 * 128:(k4 + 1) * 128], zf.t[:, k4 * 128:(k4 + 1) * 128], self.ident.t[:]),
                                            reads=[zf.b, self.ident.b], writes=[pT.b], inc=(k4 == 3))
                                    self.op(self.act, lambda: nc.scalar.activation(
                                        out=ZK.t[:, tt * 4:(tt + 1) * 4, 0, cch * 128:(cch + 1) * 128],
                                        in_=pTb[:, 0:512].rearrange("p (k c) -> p k c", k=4), func=AF.Copy),
                                        reads=[pT.b], writes=zB[tt * 4:(tt + 1) * 4])
                                else:
                                    self.op(self.dve, lambda: nc.vector.tensor_tensor(
                                        out=ya.t[:, hf * 2 + cch, tsl], in0=pO.ap, in1=x2h.t[:, cch, tsl], op=ALU.mult),
                                        reads=[pO.b, x2h.b], writes=[yaB[hf * 2 + cch][tt]])
                self.barrier()


def _t5_bucket(rel):
    half = 16
    ret = np.where(rel > 0, half, 0)
    n = np.abs(rel)
    nf = np.maximum(n, 1).astype(np.float32)
    large = 8 + (np.log(nf / np.float32(8)) / np.float32(math.log(1024 / 8)) * np.float32(half - 8)).astype(np.int32)
    large = np.minimum(large, half - 1)
    return ret + np.where(n < 8, n, large)


def _col(v):
    v = np.asarray(v, np.float32).reshape(-1, 128)
    return np.ascontiguousarray(v.T)


_CONST = {}


def _constants():
    if _CONST:
        return _CONST
    L = T
    f32 = np.float32
    t = np.linspace(0.0, 1.0, L, dtype=f32)[:, None]
    tr = np.arange(L, dtype=f32)[:, None]
    wpos = (f32(2.0 * math.pi) * tr / f32(L)).astype(f32)
    fb = np.linspace(1e-4, 15, 16, dtype=f32)[None, :]
    ang = (fb * wpos).astype(f32)
    z = np.concatenate([t, np.cos(ang.astype(np.float64)).astype(f32), -np.sin(ang.astype(np.float64)).astype(f32)], axis=-1)
    _CONST["zT"] = np.ascontiguousarray(z.T.astype(f32))
    deltas = np.abs(np.linspace(math.log(1e-2) / 0.3, math.log(1e-2) / 1.5, 512, dtype=f32))
    _CONST["delta_bc"] = np.ascontiguousarray(np.broadcast_to(deltas[None, :], (128, 512))).astype(f32)
    tt = np.linspace(0.0, 1.0, L, dtype=f32)
    _CONST["tcol"] = np.ascontiguousarray((-tt).reshape(16, 128).T).astype(f32)
    N = 2 * L
    tt_ = np.arange(L, dtype=np.int64)
    ff = np.arange(NF * 128, dtype=np.int64)
    idx = (tt_[:, None] * ff[None, :]) % N
    ang = idx.astype(np.float64) * (2.0 * math.pi / N)
    valid = (ff <= L)[None, :]
    C = np.where(valid, np.cos(ang), 0.0)
    S = np.where(valid, -np.sin(ang), 0.0)
    csf = np.stack([C, S], 0)
    csf = csf.reshape(2, 16, 128, NF, 128)
    _CONST["csf"] = np.ascontiguousarray(csf.transpose(3, 2, 0, 1, 4)).astype(ml_dtypes.bfloat16)
    scale = np.where((ff == 0) | (ff == L), 1.0 / N, 2.0 / N) * (ff <= L)
    Ci = (C * scale[None, :]).T
    Si = (S * scale[None, :]).T
    csi = np.stack([Ci, Si], 1)
    csi = csi.reshape(NF, 128, 2, 2, L // 2)
    _CONST["csi"] = np.ascontiguousarray(csi.transpose(0, 3, 1, 2, 4)).astype(ml_dtypes.bfloat16)
    _CONST["ident"] = np.eye(128, dtype=np.float32).astype(ml_dtypes.bfloat16)
    bidx = []
    for g, (w_, d) in enumerate(GROUPS):
        Ls = L // d
        off = 64 if Ls > 128 else 0
        kk = np.arange(128)[:, None]
        qq = np.arange(256)[None, :]
        delta = kk - qq + off
        ok = np.abs(delta) <= 64
        if not off:
            ok = ok & (qq < 128)
        bidx.append((_t5_bucket((delta * d).astype(np.int32)), ok))
    _CONST["bidx"] = bidx
    return _CONST


def prep_shared(inp):
    c = _constants()
    f32 = np.float32
    sh = {}
    for n in ("ffn1_wg", "ffn1_wu", "ffn1_wd", "ffn2_wg", "ffn2_wu", "ffn2_wd", "w_in", "w_gate", "w_proj_hy",
              "w_proj_attn", "w_proj_rg", "w_out", "hy_w1", "hy_w2", "hy_w3", "hy_wout"):
        sh[n] = np.ascontiguousarray(np.asarray(inp[n], f32))
    pcol = np.zeros((4, 128, NPC), f32)
    for l in range(4):
        pcol[l, :, PC["n1"]:PC["n1"] + 8] = _col(inp["ffn1_norm"][l])
        pcol[l, :, PC["nm"]:PC["nm"] + 8] = _col(inp["mix_norm"][l])
        pcol[l, :, PC["n2"]:PC["n2"] + 8] = _col(inp["ffn2_norm"][l])
        pcol[l, :, PC["bg"]:PC["bg"] + 24] = _col(inp["b_gate"][l])
        for k in range(3):
            pcol[l, :, PC["hcw"] + k * 12:PC["hcw"] + (k + 1) * 12] = _col(inp["hy_conv_w"][l, k])
        pcol[l, :, PC["hcb"]:PC["hcb"] + 12] = _col(inp["hy_conv_b"][l])
        for k in range(4):
            pcol[l, :, PC["rcw"] + k * 4:PC["rcw"] + (k + 1) * 4] = _col(inp["rg_conv_w"][l, k])
        pcol[l, :, PC["rcb"]:PC["rcb"] + 4] = _col(inp["rg_conv_b"][l])
        for dd in range(2):
            pcol[l, :, PC["rba"] + dd * 4:PC["rba"] + (dd + 1) * 4] = _col(inp["rg_ba"][l, dd])
            pcol[l, :, PC["rbx"] + dd * 4:PC["rbx"] + (dd + 1) * 4] = _col(inp["rg_bx"][l, dd])
            pcol[l, :, PC["rlam"] + dd * 4:PC["rlam"] + (dd + 1) * 4] = _col(inp["rg_lambda"][l, dd])
        pcol[l, 0:64, PC["hb1"]] = inp["hy_b1"][l]
        pcol[l, 0:64, PC["hb2"]] = inp["hy_b2"][l]
        pcol[l, 0:64, PC["hb3"]] = inp["hy_b3"][l]
        pcol[l, 0:64, PC["hfr"]] = inp["hy_freq"][l]
    sh["pcol"] = pcol
    sh["fncol"] = _col(inp["final_norm"])
    sh["zT"] = c["zT"]
    sh["delta_bc"] = c["delta_bc"]
    sh["tcol"] = c["tcol"]
    sh["skip_bc"] = np.ascontiguousarray(np.broadcast_to(np.asarray(inp["hy_skip"], f32)[:, :, None, :], (4, 2, 128, 512)))
    bd = np.zeros((4, 2, 2, 4, 128, 128), f32)
    for wi, nm in enumerate(("rg_wa", "rg_wx")):
        w = np.asarray(inp[nm], f32)
        for cc in range(4):
            bd[:, wi, :, cc, 0:64, 0:64] = w[:, :, 2 * cc]
            bd[:, wi, :, cc, 64:128, 64:128] = w[:, :, 2 * cc + 1]
    sh["rg_bd"] = bd
    rb = np.asarray(inp["rel_bias"], f32)
    ab = np.empty((3, 128, 8, 256), f32)
    for g in range(3):
        bi, ok = c["bidx"][g]
        tab = rb[:, g * 8:(g + 1) * 8][bi]
        tab = np.where(ok[:, :, None], tab, f32(-30000.0))
        ab[g] = tab.transpose(0, 2, 1)
    sh["attn_bias"] = ab
    sh["csf"] = c["csf"]
    sh["csi"] = c["csi"]
    sh["ident"] = c["ident"]
    return sh


_NC_CACHE = {}


def get_nc(n_seq, layers, flags=None, nl=4):
    key = (n_seq, tuple(layers), tuple(sorted((flags or {}).items())), nl)
    if key not in _NC_CACHE:
        b = Builder(n_seq, layers, flags, nl)
        nc = b.nc
        g = b.es
        b.eps = b.sb(g, [128, 1], F32)
        b.one = b.sb(g, [128, 1], F32)
        b.op(b.dve, lambda: nc.vector.memset(b.eps.t[:], 1e-6), writes=[b.eps.b])
        b.op(b.dve, lambda: nc.vector.memset(b.one.t[:], 1.0), writes=[b.one.b])
        b.build()
        _NC_CACHE[key] = (b, nc)
    return _NC_CACHE[key][1]


LAYER_KEYS = ("ffn1_wg", "ffn1_wu", "ffn1_wd", "ffn2_wg", "ffn2_wu", "ffn2_wd", "w_in", "w_gate", "w_proj_hy",
              "w_proj_attn", "w_proj_rg", "w_out", "hy_w1", "hy_w2", "hy_w3", "hy_wout", "pcol", "skip_bc", "rg_bd")
FUSED = True


def _pad_layer(a, l):
    return np.ascontiguousarray(a[l:l + 1])


def kernel(**inputs):
    x = np.asarray(inputs["x"], np.float32)
    B = x.shape[0]
    n_cores = 8
    per = B // n_cores
    sh = prep_shared(inputs)
    out = np.empty_like(x)
    if FUSED:
        nc = get_nc(per, [0, 1, 2, 3])
        in_maps = []
        for c in range(n_cores):
            m = dict(sh)
            m["xT"] = np.ascontiguousarray(x[c * per:(c + 1) * per].transpose(0, 2, 1))
            in_maps.append(m)
        res = run_bass_kernel_spmd(nc, in_maps, core_ids=list(range(n_cores)))
        for c in range(n_cores):
            out[c * per:(c + 1) * per] = np.asarray(res.results[c]["yT"], np.float32).transpose(0, 2, 1)
        return out
    nc_layer = get_nc(1, [0], {"final": False}, nl=1)
    nc_final = get_nc(1, [], {"final": True}, nl=1)
    shl = []
    for l in range(4):
        m = dict(sh)
        for k in LAYER_KEYS:
            m[k] = _pad_layer(sh[k], l)
        shl.append(m)
    for s in range(per):
        hT = [np.ascontiguousarray(x[c * per + s:c * per + s + 1].transpose(0, 2, 1)) for c in range(n_cores)]
        for l in range(4):
            in_maps = []
            for c in range(n_cores):
                m = dict(shl[l])
                m["xT"] = hT[c]
                in_maps.append(m)
            res = run_bass_kernel_spmd(nc_layer, in_maps, core_ids=list(range(n_cores)))
            hT = [np.ascontiguousarray(np.asarray(res.results[c]["yT"], np.float32)) for c in range(n_cores)]
        in_maps = []
        for c in range(n_cores):
            m = dict(shl[0])
            m["xT"] = hT[c]
            in_maps.append(m)
        res = run_bass_kernel_spmd(nc_final, in_maps, core_ids=list(range(n_cores)))
        for c in range(n_cores):
            out[c * per + s] = np.asarray(res.results[c]["yT"], np.float32)[0].T
    return out
```
